# Optimizing a Trainium2 kernel written in Bass

```python
import jax, jax.numpy as jnp
from jax import lax
import numpy as np

D_MODEL = 2048
BATCH = 16
SEQ = 2048
DEPTH = 4
DEC_BATCH = 4
DEC_SEQ = 4096
PAST_LEN = 128

N_META = 16
GRID_W = 64
NORM_EPS = 1e-6
N_MIXERS = 3
N_RG = (DEPTH + 2) // 3
N_NA = (DEPTH + 1) // 3
N_GDN = DEPTH // 3
CONV4_PAD_LEFT = 1

RG_WIDTH = D_MODEL
RG_BLOCK = 256
RG_BLOCKS = RG_WIDTH // RG_BLOCK
RG_CONV = 4
RG_C = 8.0

NA_HEAD_DIM = 128
NA_HEADS = D_MODEL // NA_HEAD_DIM
NA_WIN_R = 8
NA_WIN_C = 16
NA_QCB = 16
NA_KCB = NA_QCB + NA_WIN_C
NA_NCB = GRID_W // NA_QCB
NEG_INF = -1e30

GDN_DK = 128
GDN_DV = 128
GDN_QK_HEADS = D_MODEL // GDN_DK
GDN_V_HEADS = 2 * GDN_QK_HEADS
GDN_KEY_DIM = GDN_QK_HEADS * GDN_DK
GDN_VAL_DIM = GDN_V_HEADS * GDN_DV
GDN_CONV_DIM = 2 * GDN_KEY_DIM + GDN_VAL_DIM
GDN_PROJ = GDN_CONV_DIM + GDN_VAL_DIM + 4 * GDN_V_HEADS
GDN_CONV = 4
GDN_CHUNK = 64

D_FF = ((8 * D_MODEL // 3 + 255) // 256) * 256
FFN_CONV = 3

kernel_name = 'hybrid_bidir_encoder_rglru_natten_gdn'


def _rmsnorm(x, w):
    xf = x.astype(jnp.float32)
    y = xf * lax.rsqrt(jnp.mean(xf * xf, axis=-1, keepdims=True) + NORM_EPS)
    return (y * w.astype(jnp.float32)).astype(x.dtype)


def _dwconv(x, w, pad_left):
    K, T = w.shape[0], x.shape[1]
    xp = jnp.pad(x, ((0, 0), (pad_left, K - 1 - pad_left), (0, 0)))
    y = xp[:, 0:T] * w[0]
    for j in range(1, K):
        y = y + xp[:, j:j + T] * w[j]
    return y


def _linear_combine(left, right):
    a1, b1 = left
    a2, b2 = right
    return a1 * a2, a2 * b1 + b2


def _rglru_scan(xc, w_a, b_a, w_i, b_i, lam, reverse):
    B, T, W = xc.shape
    xb = xc.reshape(B, T, RG_BLOCKS, RG_BLOCK)
    r = jax.nn.sigmoid(jnp.einsum('btnc,ncd->btnd', xb, w_a.astype(jnp.float32)).reshape(B, T, W) + b_a.astype(jnp.float32))
    i = jax.nn.sigmoid(jnp.einsum('btnc,ncd->btnd', xb, w_i.astype(jnp.float32)).reshape(B, T, W) + b_i.astype(jnp.float32))
    log_a = -RG_C * r * jax.nn.softplus(-lam.astype(jnp.float32))
    a = jnp.exp(log_a)
    u = jnp.sqrt(-jnp.expm1(2.0 * log_a)) * (i * xc)
    _, hs = lax.associative_scan(_linear_combine, (a, u), axis=1, reverse=reverse)
    return hs


def _rglru_mixer(h, w_in, conv_w, conv_b, w_a, b_a, w_i, b_i, lam, w_out):
    y = h @ w_in
    gate = jax.nn.gelu(y[..., :RG_WIDTH], approximate=True)
    xc = (_dwconv(y[..., RG_WIDTH:], conv_w, CONV4_PAD_LEFT) + conv_b).astype(jnp.float32)
    hs = (_rglru_scan(xc, w_a[0], b_a[0], w_i[0], b_i[0], lam[0], False)
          + _rglru_scan(xc, w_a[1], b_a[1], w_i[1], b_i[1], lam[1], True))
    return (gate * hs.astype(h.dtype)) @ w_out


def _na_mixer(h, w_qkv, rpb, meta_bias, w_o):
    B, T, _ = h.shape
    L = T - N_META
    rows = L // GRID_W
    kh = min(NA_WIN_R, rows)
    qkv = (h @ w_qkv).reshape(B, T, 3, NA_HEADS, NA_HEAD_DIM)
    q = qkv[:, :, 0] * (NA_HEAD_DIM ** -0.5)
    k = qkv[:, :, 1]
    v = qkv[:, :, 2]
    qm, km, vm = q[:, :N_META], k[:, :N_META], v[:, :N_META]
    kg = k[:, N_META:].reshape(B, rows, GRID_W, NA_HEADS, NA_HEAD_DIM)
    vg = v[:, N_META:].reshape(B, rows, GRID_W, NA_HEADS, NA_HEAD_DIM)
    qg = q[:, N_META:].reshape(B, rows, GRID_W, NA_HEADS, NA_HEAD_DIM).transpose(1, 0, 2, 3, 4)
    mb = meta_bias.astype(jnp.float32)
    rpb_f = rpb.astype(jnp.float32)

    s_m = jnp.einsum('bqhd,bkhd->bhqk', qm, km).astype(jnp.float32) + mb[:, None, :]
    out_meta = jnp.einsum('bhqk,bkhd->bqhd', jax.nn.softmax(s_m, axis=-1).astype(v.dtype), vm)

    r_idx = np.arange(rows)
    row_start = np.clip(r_idx - NA_WIN_R // 2, 0, rows - kh)
    row_off = row_start[:, None] + np.arange(kh)[None, :] - r_idx[:, None] + NA_WIN_R - 1
    cols = np.arange(GRID_W)
    col_start = np.clip(cols - NA_WIN_C // 2, 0, GRID_W - NA_WIN_C).reshape(NA_NCB, NA_QCB)
    kblk_start = [int(np.clip(b * NA_QCB - NA_WIN_C // 2, 0, GRID_W - NA_KCB)) for b in range(NA_NCB)]
    key_col = np.array(kblk_start)[:, None] + np.arange(NA_KCB)[None, :]
    q_col = cols.reshape(NA_NCB, NA_QCB)
    col_valid = jnp.asarray((key_col[:, None, :] >= col_start[..., None]) & (key_col[:, None, :] < col_start[..., None] + NA_WIN_C))
    col_off = jnp.asarray(np.clip(key_col[:, None, :] - q_col[..., None], -(NA_WIN_C - 1), NA_WIN_C - 1) + NA_WIN_C - 1, jnp.int32)

    def row_attend(args):
        q_r, rs_r, roff_r = args
        k_r = lax.dynamic_slice_in_dim(kg, rs_r, kh, axis=1)
        v_r = lax.dynamic_slice_in_dim(vg, rs_r, kh, axis=1)
        k_b = jnp.stack([k_r[:, :, s:s + NA_KCB] for s in kblk_start], axis=1)
        v_b = jnp.stack([v_r[:, :, s:s + NA_KCB] for s in kblk_start], axis=1)
        q_b = q_r.reshape(B, NA_NCB, NA_QCB, NA_HEADS, NA_HEAD_DIM)
        s_loc = jnp.einsum('bnqhd,bnrkhd->bhnqrk', q_b, k_b).astype(jnp.float32)
        bias = rpb_f[:, roff_r[None, None, :, None], col_off[:, :, None, :]]
        s_loc = jnp.where(col_valid[:, :, None, :], s_loc + bias, NEG_INF)
        s_met = jnp.einsum('bnqhd,bmhd->bhnqm', q_b, km).astype(jnp.float32) + mb[:, None, None, :]
        s_all = jnp.concatenate([s_loc.reshape(B, NA_HEADS, NA_NCB, NA_QCB, kh * NA_KCB), s_met], axis=-1)
        p = jax.nn.softmax(s_all, axis=-1).astype(v.dtype)
        p_loc = p[..., :kh * NA_KCB].reshape(B, NA_HEADS, NA_NCB, NA_QCB, kh, NA_KCB)
        p_met = p[..., kh * NA_KCB:]
        o = jnp.einsum('bhnqrk,bnrkhd->bnqhd', p_loc, v_b) + jnp.einsum('bhnqm,bmhd->bnqhd', p_met, vm)
        return o.reshape(B, GRID_W, NA_HEADS, NA_HEAD_DIM)

    out_grid = lax.map(row_attend, (qg, jnp.asarray(row_start, jnp.int32), jnp.asarray(row_off, jnp.int32)))
    out_grid = out_grid.transpose(1, 0, 2, 3, 4).reshape(B, L, NA_HEADS, NA_HEAD_DIM)
    o = jnp.concatenate([out_meta, out_grid], axis=1).reshape(B, T, NA_HEADS * NA_HEAD_DIM)
    return o @ w_o


def _l2norm(x):
    return x * lax.rsqrt(jnp.sum(x * x, axis=-1, keepdims=True) + 1e-6)


def _chunk_gated_delta(q, k, v, beta, g):
    B, T, H, DK = q.shape
    DV = v.shape[-1]
    C = GDN_CHUNK
    N = T // C

    def to_chunks(t):
        return jnp.moveaxis(t.reshape((B, N, C) + t.shape[2:]), 3, 1)

    q = to_chunks(q) * (DK ** -0.5)
    k = to_chunks(k)
    v = to_chunks(v)
    beta = to_chunks(beta)
    g = jnp.cumsum(to_chunks(g), axis=-1)
    incl = jnp.tril(jnp.ones((C, C), dtype=bool))
    strict = jnp.tril(jnp.ones((C, C), dtype=bool), -1)
    decay = jnp.where(incl, jnp.exp(jnp.where(incl, g[..., :, None] - g[..., None, :], 0.0)), 0.0)
    kk = jnp.einsum('bhncd,bhnsd->bhncs', k, k)
    a_mat = jnp.where(strict, kk * decay, 0.0) * beta[..., :, None] + jnp.eye(C, dtype=q.dtype)
    rhs = jnp.concatenate([v * beta[..., None], k * (beta * jnp.exp(g))[..., None]], axis=-1)
    sol = lax.linalg.triangular_solve(a_mat, rhs, left_side=True, lower=True, unit_diagonal=True)
    u, w = sol[..., :DV], sol[..., DV:]
    qk = jnp.where(incl, jnp.einsum('bhncd,bhnsd->bhncs', q, k) * decay, 0.0)
    g_last = g[..., -1]
    k_st = k * jnp.exp(g_last[..., None] - g)[..., None]
    q_st = q * jnp.exp(g)[..., None]
    xs = tuple(jnp.moveaxis(t, 2, 0) for t in (q_st, k_st, u, w, qk, g_last))

    def step(S, inp):
        q_i, k_i, u_i, w_i, qk_i, gl_i = inp
        v_new = u_i - jnp.einsum('bhcd,bhde->bhce', w_i, S)
        o_i = jnp.einsum('bhcd,bhde->bhce', q_i, S) + jnp.einsum('bhcs,bhse->bhce', qk_i, v_new)
        S = S * jnp.exp(gl_i)[..., None, None] + jnp.einsum('bhcd,bhce->bhde', k_i, v_new)
        return S, o_i

    S0 = jnp.zeros((B, H, DK, DV), jnp.float32)
    _, o = lax.scan(step, S0, xs)
    return o.transpose(1, 0, 3, 2, 4).reshape(B, T, H, DV)


def _gdn_mixer(h, w_in, conv_w, a_log, dt_bias, norm_w, w_out):
    B, T, _ = h.shape
    proj = h @ w_in
    qkv = jax.nn.silu(_dwconv(proj[..., :GDN_CONV_DIM], conv_w, CONV4_PAD_LEFT)).astype(jnp.float32)
    z = proj[..., GDN_CONV_DIM:GDN_CONV_DIM + GDN_VAL_DIM].astype(jnp.float32).reshape(B, T, GDN_V_HEADS, GDN_DV)
    ba = proj[..., GDN_CONV_DIM + GDN_VAL_DIM:].astype(jnp.float32).reshape(B, T, 2, 2, GDN_V_HEADS)
    beta = jax.nn.sigmoid(ba[:, :, 0])
    g = -jnp.exp(a_log.astype(jnp.float32)) * jax.nn.softplus(ba[:, :, 1] + dt_bias.astype(jnp.float32))
    q = _l2norm(qkv[..., :GDN_KEY_DIM].reshape(B, T, GDN_QK_HEADS, GDN_DK))
    k = _l2norm(qkv[..., GDN_KEY_DIM:2 * GDN_KEY_DIM].reshape(B, T, GDN_QK_HEADS, GDN_DK))
    v = qkv[..., 2 * GDN_KEY_DIM:].reshape(B, T, GDN_V_HEADS, GDN_DV)
    rep = GDN_V_HEADS // GDN_QK_HEADS
    q = jnp.repeat(q, rep, axis=2)
    k = jnp.repeat(k, rep, axis=2)
    pad = (-T) % GDN_CHUNK

    def pad_t(t):
        return jnp.pad(t, [(0, 0), (pad, 0)] + [(0, 0)] * (t.ndim - 2))

    def flip(t):
        return jnp.flip(t, axis=1)

    q, k, v, beta, g = pad_t(q), pad_t(k), pad_t(v), pad_t(beta), pad_t(g)
    o_f = _chunk_gated_delta(q, k, v, beta[:, :, 0], g[:, :, 0])
    o_b = flip(_chunk_gated_delta(flip(q), flip(k), flip(v), flip(beta[:, :, 1]), flip(g[:, :, 1])))
    o = (o_f + o_b)[:, pad:]
    o = o * lax.rsqrt(jnp.mean(o * o, axis=-1, keepdims=True) + NORM_EPS) * norm_w.astype(jnp.float32) * jax.nn.silu(z)
    return o.reshape(B, T, GDN_VAL_DIM).astype(h.dtype) @ w_out


def _conv_ffn(h, w_gate, w_up, conv_w, conv_b, w_down):
    a = _dwconv(h @ w_gate, conv_w, FFN_CONV // 2) + conv_b
    return (jax.nn.silu(a) * (h @ w_up)) @ w_down


def _trunk(x, meta_tokens, mix_norm, ffn_norm, final_norm,
           rg_w_in, rg_conv_w, rg_conv_b, rg_w_a, rg_b_a, rg_w_i, rg_b_i, rg_lam, rg_w_out,
           na_w_qkv, na_rpb, na_meta_bias, na_w_o,
           gdn_w_in, gdn_conv_w, gdn_a_log, gdn_dt_bias, gdn_norm_w, gdn_w_out,
           ffn_w_gate, ffn_w_up, ffn_conv_w, ffn_conv_b, ffn_w_down):
    B = x.shape[0]
    meta = jnp.broadcast_to(meta_tokens.astype(x.dtype)[None], (B, N_META, D_MODEL))
    h = jnp.concatenate([meta, x], axis=1)
    for i in range(DEPTH):
        kind, j = i % N_MIXERS, i // N_MIXERS
        hn = _rmsnorm(h, mix_norm[i])
        if kind == 0:
            mix = _rglru_mixer(hn, rg_w_in[j], rg_conv_w[j], rg_conv_b[j], rg_w_a[j], rg_b_a[j],
                               rg_w_i[j], rg_b_i[j], rg_lam[j], rg_w_out[j])
        elif kind == 1:
            mix = _na_mixer(hn, na_w_qkv[j], na_rpb[j], na_meta_bias[j], na_w_o[j])
        else:
            mix = _gdn_mixer(hn, gdn_w_in[j], gdn_conv_w[j], gdn_a_log[j], gdn_dt_bias[j],
                             gdn_norm_w[j], gdn_w_out[j])
        h = h + mix
        h = h + _conv_ffn(_rmsnorm(h, ffn_norm[i]), ffn_w_gate[i], ffn_w_up[i], ffn_conv_w[i],
                          ffn_conv_b[i], ffn_w_down[i])
    return _rmsnorm(h, final_norm)[:, N_META:]


def setup_inputs(seed: int = 0) -> dict:
    key = jax.random.key(seed)
    ks = iter(jax.random.split(key, 40))

    def nrm(shape, scale):
        return jax.random.normal(next(ks), shape, jnp.float32) * scale

    def gain(shape):
        return 1.0 + nrm(shape, 0.02)

    u_lru = jax.random.uniform(next(ks), (N_RG, 2, RG_WIDTH), jnp.float32, 0.9, 0.999)
    a_lru = u_lru ** (1.0 / RG_C)
    rg_lam = jnp.log(a_lru) - jnp.log1p(-a_lru)
    gdn_a_log = jnp.log(jax.random.uniform(next(ks), (N_GDN, 2, GDN_V_HEADS), jnp.float32, 1.0, 16.0))
    return {
        'x_prompt': nrm((BATCH, SEQ, D_MODEL), 1.0),
        'x_sample': nrm((DEC_BATCH, DEC_SEQ, D_MODEL), 1.0),
        'meta_tokens': nrm((N_META, D_MODEL), 1.0),
        'mix_norm': gain((DEPTH, D_MODEL)),
        'ffn_norm': gain((DEPTH, D_MODEL)),
        'final_norm': gain((D_MODEL,)),
        'rg_w_in': nrm((N_RG, D_MODEL, 2 * RG_WIDTH), D_MODEL ** -0.5),
        'rg_conv_w': nrm((N_RG, RG_CONV, RG_WIDTH), RG_CONV ** -0.5),
        'rg_conv_b': nrm((N_RG, RG_WIDTH), 0.02),
        'rg_w_a': nrm((N_RG, 2, RG_BLOCKS, RG_BLOCK, RG_BLOCK), RG_BLOCK ** -0.5),
        'rg_b_a': nrm((N_RG, 2, RG_WIDTH), 0.02),
        'rg_w_i': nrm((N_RG, 2, RG_BLOCKS, RG_BLOCK, RG_BLOCK), RG_BLOCK ** -0.5),
        'rg_b_i': nrm((N_RG, 2, RG_WIDTH), 0.02),
        'rg_lam': rg_lam,
        'rg_w_out': nrm((N_RG, RG_WIDTH, D_MODEL), RG_WIDTH ** -0.5),
        'na_w_qkv': nrm((N_NA, D_MODEL, 3 * NA_HEADS * NA_HEAD_DIM), D_MODEL ** -0.5),
        'na_rpb': nrm((N_NA, NA_HEADS, 2 * NA_WIN_R - 1, 2 * NA_WIN_C - 1), 0.1),
        'na_meta_bias': nrm((N_NA, NA_HEADS, N_META), 0.1),
        'na_w_o': nrm((N_NA, NA_HEADS * NA_HEAD_DIM, D_MODEL), (NA_HEADS * NA_HEAD_DIM) ** -0.5),
        'gdn_w_in': nrm((N_GDN, D_MODEL, GDN_PROJ), D_MODEL ** -0.5),
        'gdn_conv_w': nrm((N_GDN, GDN_CONV, GDN_CONV_DIM), GDN_CONV ** -0.5),
        'gdn_a_log': gdn_a_log,
        'gdn_dt_bias': 1.0 + nrm((N_GDN, 2, GDN_V_HEADS), 0.1),
        'gdn_norm_w': gain((N_GDN, GDN_DV)),
        'gdn_w_out': nrm((N_GDN, GDN_VAL_DIM, D_MODEL), GDN_VAL_DIM ** -0.5),
        'ffn_w_gate': nrm((DEPTH, D_MODEL, D_FF), D_MODEL ** -0.5),
        'ffn_w_up': nrm((DEPTH, D_MODEL, D_FF), D_MODEL ** -0.5),
        'ffn_conv_w': nrm((DEPTH, FFN_CONV, D_FF), FFN_CONV ** -0.5),
        'ffn_conv_b': nrm((DEPTH, D_FF), 0.02),
        'ffn_w_down': nrm((DEPTH, D_FF, D_MODEL), D_FF ** -0.5),
    }


def reference(x_prompt, x_sample, meta_tokens, mix_norm, ffn_norm, final_norm,
              rg_w_in, rg_conv_w, rg_conv_b, rg_w_a, rg_b_a, rg_w_i, rg_b_i, rg_lam, rg_w_out,
              na_w_qkv, na_rpb, na_meta_bias, na_w_o,
              gdn_w_in, gdn_conv_w, gdn_a_log, gdn_dt_bias, gdn_norm_w, gdn_w_out,
              ffn_w_gate, ffn_w_up, ffn_conv_w, ffn_conv_b, ffn_w_down):
    weights = (meta_tokens, mix_norm, ffn_norm, final_norm,
               rg_w_in, rg_conv_w, rg_conv_b, rg_w_a, rg_b_a, rg_w_i, rg_b_i, rg_lam, rg_w_out,
               na_w_qkv, na_rpb, na_meta_bias, na_w_o,
               gdn_w_in, gdn_conv_w, gdn_a_log, gdn_dt_bias, gdn_norm_w, gdn_w_out,
               ffn_w_gate, ffn_w_up, ffn_conv_w, ffn_conv_b, ffn_w_down)
    y_prompt = _trunk(x_prompt, *weights)
    y_sample = _trunk(x_sample, *weights)
    return (y_prompt, y_sample)
```

```python
import numpy as np
from contextlib import ExitStack
import concourse.bass as bass
import concourse.mybir as mybir
from concourse.bass_utils import run_bass_kernel_spmd

F32 = mybir.dt.float32
BF16 = mybir.dt.bfloat16
AF = mybir.ActivationFunctionType
ALU = mybir.AluOpType
AX = mybir.AxisListType

D = 2048
KD = D // 128
N_META = 16
GRID_W = 64
DFF = 5632
KF = DFF // 128
EPS = 1e-6
DEPTH = 4
NSEM = 60
RG_STAGES = 3
SKIP_FFN = False
RG_IN_PARTS = "both"
PH_SKIP = set()
DEBUG_GDN = False
DEBUG_SNAP = False
LAST = {}
MIXERS = None


class Ctr:
    def __init__(self, sem, step):
        self.sem = sem
        self.step = step
        self.n = 0


class Buf:
    def __init__(self, name=""):
        self.name = name
        self.lw = None
        self.rd = {}


class Phase:
    def __init__(self, K, name):
        self.K = K
        self.nc = K.nc
        K.nph = getattr(K, "nph", 0) + 1
        self.name = f"{name}p{K.nph}"
        self.sems = list(K.sems)
        self.used = []
        self.ctr = {e: self._newctr(1) for e in ("pe", "dve", "act", "pool")}
        self.streams = {e: [] for e in ("pe", "dve", "act", "pool", "sp")}
        self.waited = {e: {} for e in ("pe", "dve", "act", "pool", "sp")}
        self.dctrs = []
        self.stack = ExitStack()
        self.nt = 0

    def _newctr(self, step):
        sem = self.sems.pop()
        self.used.append(sem)
        return Ctr(sem, step)

    def dctr(self):
        c = self._newctr(16)
        self.dctrs.append(c)
        return c

    def sb(self, name, shape, dt):
        self.nt += 1
        return self.stack.enter_context(self.nc.sbuf_tensor(f"{self.name}_{name}_{self.nt}", list(shape), dt))

    def ps(self, name, shape, dt=F32):
        self.nt += 1
        return self.stack.enter_context(self.nc.psum_tensor(f"{self.name}_{name}_{self.nt}", list(shape), dt))

    def _waits(self, eng, reads, writes, own):
        need = {}
        for b in reads:
            if b.lw is not None:
                c, n = b.lw
                need[c] = max(need.get(c, 0), n)
        for b in writes:
            if b.lw is not None:
                c, n = b.lw
                if c is not own:
                    need[c] = max(need.get(c, 0), n)
            for c, n in b.rd.items():
                if c is not own:
                    need[c] = max(need.get(c, 0), n)
        w = self.waited[eng]
        out = []
        for c, n in need.items():
            if w.get(c, 0) < n:
                w[c] = n
                out.append((c.sem, n * c.step))
        return out

    def _mark(self, c, n, reads, writes):
        for b in reads:
            b.rd[c] = n
        for b in writes:
            b.lw = (c, n)
            b.rd = {}

    def op(self, eng, fn, reads=(), writes=()):
        c = self.ctr[eng]
        waits = self._waits(eng, reads, writes, c)
        c.n += 1
        n = c.n
        assert n < 65000, (self.name, eng)
        sem = c.sem

        def run(e):
            for s, v in waits:
                e.wait_ge(s, v)
            fn(e).then_inc(sem, 1)

        self.streams[eng].append(run)
        self._mark(c, n, reads, writes)

    def mm(self, fns, reads=(), writes=()):
        c = self.ctr["pe"]
        waits = self._waits("pe", reads, writes, c)
        c.n += 1
        n = c.n
        assert n < 65000, (self.name, "pe")
        sem = c.sem

        def run(e):
            for s, v in waits:
                e.wait_ge(s, v)
            for f in fns[:-1]:
                f(e)
            fns[-1](e).then_inc(sem, 1)

        self.streams["pe"].append(run)
        self._mark(c, n, reads, writes)

    def dma(self, q, out, in_, ctr, reads=(), writes=()):
        waits = self._waits(q, reads, writes, None)
        ctr.n += 1
        n = ctr.n
        assert n * 16 < 65000, (self.name, "dma")
        sem = ctr.sem

        def run(e):
            for s, v in waits:
                e.wait_ge(s, v)
            e.dma_start(out=out, in_=in_).then_inc(sem, 16)

        self.streams[q].append(run)
        self._mark(ctr, n, reads, writes)

    def finish(self):
        finals = [(c.sem, c.n * 16) for c in self.dctrs if c.n > 0]
        st = self.streams
        with self.nc.allow_non_contiguous_dma("feature-major scratch / tiny parameter vectors"), \
                self.nc.Block() as block:
            @block.sync
            def _(e):
                for f in st["sp"]:
                    f(e)
                for s, v in finals:
                    e.wait_ge(s, v)

            @block.tensor
            def _(e):
                for f in st["pe"]:
                    f(e)

            @block.vector
            def _(e):
                for f in st["dve"]:
                    f(e)

            @block.scalar
            def _(e):
                for f in st["act"]:
                    f(e)

            @block.gpsimd
            def _(e):
                for f in st["pool"]:
                    f(e)
        for sem in self.used:
            self.nc.gpsimd.sem_clear(sem)
        self.nc.all_engine_barrier()
        self.stack.close()


def v4_of(ap3):
    a = ap3.ap
    return bass.AP(ap3.tensor, ap3.offset, [list(a[0]), [2 * a[1][0], 4], [a[1][0], 2], list(a[2])])


class Rot:
    def __init__(self, ph, name, shape, dt, n, dma=True, psum=False):
        self.slots = []
        for i in range(n):
            t = ph.ps(f"{name}{i}", shape, dt) if psum else ph.sb(f"{name}{i}", shape, dt)
            self.slots.append((t, Buf(f"{name}{i}"), ph.dctr() if dma else None))
        self.i = 0

    def next(self):
        s = self.slots[self.i % len(self.slots)]
        self.i += 1
        return s


def tiles_of(T, wmax):
    n = (T + wmax - 1) // wmax
    base = T // n
    rem = T - base * n
    out = []
    t = 0
    for i in range(n):
        w = base + (1 if i < rem else 0)
        out.append((t, w))
        t += w
    return out


SHAPES = {
    "meta_tokens": [N_META, D], "mix_norm": [DEPTH, D], "ffn_norm": [DEPTH, D], "final_norm": [D],
    "rg_w_in": [2, D, 2 * D], "rg_conv_w": [2, 4, D], "rg_conv_b": [2, D], "rg_w_a": [2, 2, 8, 256, 256],
    "rg_b_a": [2, 2, D], "rg_w_i": [2, 2, 8, 256, 256], "rg_b_i": [2, 2, D], "rg_lam": [2, 2, D],
    "rg_w_out": [2, D, D], "na_w_qkv": [1, D, 3 * D], "na_rpb": [1, 16, 15, 31], "na_meta_bias": [1, 16, 16],
    "na_w_o": [1, D, D], "gdn_w_in": [1, D, 12416], "gdn_conv_w": [1, 4, 8192], "gdn_a_log": [1, 2, 32],
    "gdn_dt_bias": [1, 2, 32], "gdn_norm_w": [1, 128], "gdn_w_out": [1, 4096, D],
    "ffn_w_gate": [DEPTH, D, DFF], "ffn_w_up": [DEPTH, D, DFF], "ffn_conv_w": [DEPTH, 3, DFF],
    "ffn_conv_b": [DEPTH, DFF], "ffn_w_down": [DEPTH, DFF, D]}
PER_LAYER = {"rg_w_in", "rg_w_a", "rg_w_i", "rg_w_out", "na_w_qkv", "na_w_o", "gdn_w_in", "gdn_w_out",
             "ffn_w_gate", "ffn_w_up", "ffn_w_down"}


class LazyInputs:
    def __init__(self, nc, extra):
        self.nc = nc
        self.extra = extra
        self.d = {}

    def _decl(self, key, shape):
        if key not in self.d:
            self.d[key] = self.nc.dram_tensor(key.replace("@", "_L"), list(shape), F32, kind="ExternalInput").ap()
        return self.d[key]

    def __getitem__(self, name):
        if name in self.extra:
            return self._decl(name, self.extra[name])
        if name in PER_LAYER:
            return _PerLayer(self, name)
        return self._decl(name, SHAPES[name])


class _PerLayer:
    def __init__(self, li, name):
        self.li = li
        self.name = name

    def __getitem__(self, i):
        return self.li._decl(f"{self.name}@{i}", SHAPES[self.name][1:])


class Builder:
    def __init__(self, seq_lens, n_prompt, stop_after=None):
        self.L = list(seq_lens)
        self.T = [l + N_META for l in self.L]
        self.n_prompt = n_prompt
        self.toff = np.concatenate([[0], np.cumsum(self.T)]).astype(int).tolist()
        self.TT = self.toff[-1]
        self.GAP = 2
        self.goff = []
        o = self.GAP
        for t in self.T:
            self.goff.append(o)
            o += t + 2 * self.GAP
        self.TTG = o
        self.stop_after = stop_after

    def declare(self):
        nc = self.nc
        L = self.L
        npm = self.n_prompt
        self.inp = LazyInputs(nc, {
            "x_p": [npm, L[0], D], "x_s": [len(L) - npm, L[-1], D], "c_ident": [128, 128], "c_namask": [64, 64],
            "c_tril": [128, 128], "c_triu": [128, 128]})
        self.y_p = nc.dram_tensor("y_p", [npm, L[0], D], F32, kind="ExternalOutput").ap()
        self.y_s = nc.dram_tensor("y_s", [len(L) - npm, L[-1], D], F32, kind="ExternalOutput").ap()
        self.hT = nc.dram_tensor("s_hT", [D, self.TT], F32, kind="Internal").ap()
        self.xnT = nc.dram_tensor("s_xnT", [D, self.TTG], BF16, kind="Internal").ap()
        self.HT = nc.dram_tensor("s_HT", [DFF, self.TT], BF16, kind="Internal").ap()
        self._lazy_scratch = {
            "gT": ([D, self.TT], BF16), "xcbT": ([D, self.TT], BF16), "mT": ([D, self.TT], BF16),
            "XC": ([D, self.TT], F32), "Vtm": ([self.TT, D], BF16), "rpbpad": ([240, 160], F32)}

    def __getattr__(self, name):
        ls = self.__dict__.get("_lazy_scratch", {})
        if name in ls:
            shape, dt = ls[name]
            ap = self.nc.dram_tensor("s_" + name, list(shape), dt, kind="Internal").ap()
            self.__dict__[name] = ap
            return ap
        raise AttributeError(name)

    def x_of(self, s):
        if s < self.n_prompt:
            return self.inp["x_p"][s]
        return self.inp["x_s"][s - self.n_prompt]

    def y_of(self, s):
        if s < self.n_prompt:
            return self.y_p[s]
        return self.y_s[s - self.n_prompt]

    def load_consts(self, ph):
        ident = ph.sb("ident", [128, 128], F32)
        ones = ph.sb("ones", [128, 128], F32)
        b_id, b_on = Buf("ident"), Buf("ones")
        c = ph.dctr()
        ph.dma("sp", ident[:], self.inp["c_ident"], c, writes=[b_id])
        ph.op("pool", lambda e: e.memset(ones[:], 1.0), writes=[b_on])
        return ident, b_id, ones, b_on

    def load_cols(self, ph, name, src_rows, nk):
        R = src_rows.shape[0]
        t = ph.sb(name, [128, R, nk], F32)
        b = Buf(name)
        c = ph.dctr()
        with self.nc.allow_non_contiguous_dma("tiny per-channel parameter vectors"):
            ph.dma("sp", t[:], src_rows.rearrange("r (k p) -> p r k", p=128), c, writes=[b])
        return t, b

    def wstage(self, ph, n=3):
        return Rot(ph, "wst", [128, 4, 512], F32, n)

    def wload(self, ph, stg, dst, bdst, src, nk, N):
        for k0 in range(0, nk, 4):
            kk = min(4, nk - k0)
            st, bst, cst = stg.next()
            ph.dma("sp", st[:, 0:kk, 0:N], src[:, k0:k0 + kk, :], cst, writes=[bst])
            ph.op("pool", lambda e, st=st, k0=k0, kk=kk: e.tensor_copy(
                out=dst[:, k0:k0 + kk, :], in_=st[:, 0:kk, 0:N]), reads=[bst], writes=[bdst])

    def phase_embed(self):
        ph = Phase(self, "emb")
        ident, b_id, ones, b_on = self.load_consts(ph)
        xin = Rot(ph, "xin", [128, D], F32, 2)
        pst = Rot(ph, "pt", [128, 4, 128], F32, 4, dma=False, psum=True)
        hout = Rot(ph, "ho", [128, KD, 128], F32, 2)
        z = ph.sb("z", [128, KD, 2 * self.GAP], BF16)
        bz = Buf("z")
        cz = ph.dctr()
        ph.op("pool", lambda e: e.memset(z[:], 0.0), writes=[bz])
        xv = self.xnT.rearrange("(k p) t -> p k t", p=128)
        edges = [0] + [self.goff[s] + self.T[s] for s in range(len(self.T))]
        with self.nc.allow_non_contiguous_dma("gap zero fill"):
            for i, e0 in enumerate(edges):
                wz = self.GAP if (i == 0 or i == len(edges) - 1) else 2 * self.GAP
                ph.dma("sp", xv[:, :, e0:e0 + wz], z[:, :, 0:wz], cz, reads=[bz])
        hv = self.hT.rearrange("(k p) t -> p k t", p=128)
        for s in range(len(self.T)):
            T = self.T[s]
            for t0 in range(0, T, 128):
                w = min(128, T - t0)
                xt, bx, cx = xin.next()
                if t0 == 0:
                    ph.dma("sp", xt[0:N_META, :], self.inp["meta_tokens"], cx, writes=[bx])
                    ph.dma("sp", xt[N_META:w, :], self.x_of(s)[0:w - N_META, :], cx, writes=[bx])
                else:
                    ph.dma("sp", xt[0:w, :], self.x_of(s)[t0 - N_META:t0 - N_META + w, :], cx, writes=[bx])
                ht, bh, ch = hout.next()
                for q in range(4):
                    pt, bp, _ = pst.next()
                    fns = []
                    for j in range(4):
                        k = q * 4 + j
                        fns.append(lambda e, pt=pt, j=j, k=k, xt=xt, w=w: e.transpose(
                            out=pt[:, j, 0:w], in_=xt[0:w, k * 128:(k + 1) * 128], identity=ident[0:w, 0:w]))
                    ph.mm(fns, reads=[bx, b_id], writes=[bp])
                    eng = "dve" if q % 2 == 0 else "act"
                    if eng == "dve":
                        ph.op("dve", lambda e, pt=pt, ht=ht, q=q, w=w: e.tensor_copy(
                            out=ht[:, q * 4:q * 4 + 4, 0:w], in_=pt[:, :, 0:w]), reads=[bp], writes=[bh])
                    else:
                        ph.op("act", lambda e, pt=pt, ht=ht, q=q, w=w: e.copy(
                            out=ht[:, q * 4:q * 4 + 4, 0:w], in_=pt[:, :, 0:w]), reads=[bp], writes=[bh])
                c0 = self.toff[s] + t0
                with self.nc.allow_non_contiguous_dma("feature-major scratch store"):
                    ph.dma("sp", hv[:, :, c0:c0 + w], ht[:, :, 0:w], ch, reads=[bh])
        ph.finish()

    def phase_norm(self, wrow, final=False):
        ph = Phase(self, "nrm")
        ident, b_id, ones, b_on = self.load_consts(ph)
        wt, bw = self.load_cols(ph, "nw", wrow, KD)
        W = 512 if not final else 128
        hin = Rot(ph, "hin", [128, KD, W], F32, 2)
        sq = Rot(ph, "sq", [128, KD, W], F32, 1, dma=False)
        pss = Rot(ph, "pss", [128, 512], F32, 2, dma=False, psum=True)
        rs = Rot(ph, "rs", [128, W], F32, 2, dma=False)
        hv = self.hT.rearrange("(k p) t -> p k t", p=128)
        xv = self.xnT.rearrange("(k p) t -> p k t", p=128)
        if not final:
            xo = Rot(ph, "xo", [128, KD, W], BF16, 2)
        else:
            xo = Rot(ph, "xo", [128, KD, W], F32, 2, dma=False)
            pst = Rot(ph, "pt", [128, 4, 128], F32, 4, dma=False, psum=True)
            yo = Rot(ph, "yo", [128, D], F32, 2)
        for s in range(len(self.T)):
            T = self.T[s]
            if final:
                tl = [(N_META + i, min(128, T - N_META - i)) for i in range(0, T - N_META, 128)]
            else:
                tl = tiles_of(T, W)
            for (t0, w) in tl:
                ht, bh, ch = hin.next()
                c0 = self.toff[s] + t0
                with self.nc.allow_non_contiguous_dma("feature-major scratch load"):
                    ph.dma("sp", ht[:, :, 0:w], hv[:, :, c0:c0 + w], ch, writes=[bh])
                sqt, bs, _ = sq.next()
                ph.op("act", lambda e, sqt=sqt, ht=ht, w=w: e.activation(
                    out=sqt[:, :, 0:w], in_=ht[:, :, 0:w], func=AF.Square), reads=[bh], writes=[bs])
                pt, bp, _ = pss.next()
                fns = [lambda e, k=k, pt=pt, sqt=sqt, w=w: e.matmul(
                    pt[:, 0:w], lhsT=ones[:], rhs=sqt[:, k, 0:w], start=(k == 0), stop=(k == KD - 1))
                    for k in range(KD)]
                ph.mm(fns, reads=[bs, b_on], writes=[bp])
                rt, br, _ = rs.next()
                ph.op("act", lambda e, rt=rt, pt=pt, w=w: e.activation(
                    out=rt[:, 0:w], in_=pt[:, 0:w], func=AF.Sqrt, scale=1.0 / D, bias=self.eps_ap(ph)),
                    reads=[bp, self.b_eps], writes=[br])
                ph.op("dve", lambda e, rt=rt, w=w: e.reciprocal(out=rt[:, 0:w], in_=rt[:, 0:w]),
                      reads=[br], writes=[br])
                xt, bx, cx = xo.next()
                for k in range(KD):
                    ph.op("dve", lambda e, k=k, xt=xt, ht=ht, rt=rt, w=w: e.scalar_tensor_tensor(
                        out=xt[:, k, 0:w], in0=ht[:, k, 0:w], scalar=wt[:, 0, k:k + 1], in1=rt[:, 0:w],
                        op0=ALU.mult, op1=ALU.mult), reads=[bh, br, bw], writes=[bx])
                if not final:
                    g0 = self.goff[s] + t0
                    with self.nc.allow_non_contiguous_dma("feature-major scratch store"):
                        ph.dma("sp", xv[:, :, g0:g0 + w], xt[:, :, 0:w], cx, reads=[bx])
                else:
                    yt, by, cy = yo.next()
                    for q in range(4):
                        pt2, bp2, _ = pst.next()
                        fns = []
                        for j in range(4):
                            k = q * 4 + j
                            fns.append(lambda e, pt2=pt2, j=j, k=k, xt=xt, w=w: e.transpose(
                                out=pt2[0:w, j, :], in_=xt[:, k, 0:w], identity=ident[:]))
                        ph.mm(fns, reads=[bx, b_id], writes=[bp2])
                        if q % 2 == 0:
                            ph.op("dve", lambda e, pt2=pt2, yt=yt, q=q, w=w: e.tensor_copy(
                                out=yt[0:w, q * 512:(q + 1) * 512],
                                in_=pt2[0:w, :, :]), reads=[bp2], writes=[by])
                        else:
                            ph.op("act", lambda e, pt2=pt2, yt=yt, q=q, w=w: e.copy(
                                out=yt[0:w, q * 512:(q + 1) * 512],
                                in_=pt2[0:w, :, :]), reads=[bp2], writes=[by])
                    r0 = t0 - N_META
                    ph.dma("sp", self.y_of(s)[r0:r0 + w, :], yt[0:w, :], cy, reads=[by])
        ph.finish()

    def phase_snap(self, name):
        snap = self.nc.dram_tensor("s_snap_" + name, [D, self.TT], F32, kind="ExternalOutput").ap()
        ph = Phase(self, "snp")
        c = ph.dctr()
        for k in range(KD):
            ph.dma("sp", snap[k * 128:(k + 1) * 128, :], self.hT[k * 128:(k + 1) * 128, :], c)
        ph.finish()

    def eps_ap(self, ph):
        return self._eps[:, 0:1]

    def make_eps(self, ph):
        self._eps = ph.sb("eps", [128, 1], F32)
        self.b_eps = Buf("eps")
        ph.op("pool", lambda e: e.memset(self._eps[:], EPS), writes=[self.b_eps])

    def phase_ffn_a(self, li):
        ph = Phase(self, f"fa{li}")
        G = 4
        cw, bcw = self.load_cols(ph, "cw", self.inp["ffn_conv_w"][li], KF)
        cb, bcb = self.load_cols(ph, "cb", self.inp["ffn_conv_b"][li:li + 1], KF)
        wg = Rot(ph, "wg", [128, KD, G * 128], BF16, 2, dma=False)
        wu = Rot(ph, "wu", [128, KD, G * 128], BF16, 2, dma=False)
        wst = self.wstage(ph)
        xin = Rot(ph, "xin", [128, KD, 512], BF16, 2)
        psg = Rot(ph, "psg", [128, 512], F32, 3, dma=False, psum=True)
        psu = Rot(ph, "psu", [128, 512], F32, 3, dma=False, psum=True)
        gs = Rot(ph, "gs", [128, 512], F32, 2, dma=False)
        t1 = Rot(ph, "t1", [128, 512], F32, 2, dma=False)
        sg = Rot(ph, "sg", [128, 512], F32, 2, dma=False)
        ho = Rot(ph, "ho", [128, G, 512], BF16, 2)
        xv = self.xnT.rearrange("(k p) t -> p k t", p=128)
        Wg = self.inp["ffn_w_gate"][li].rearrange("(k p) n -> p k n", p=128)
        Wu = self.inp["ffn_w_up"][li].rearrange("(k p) n -> p k n", p=128)
        Hv = self.HT.rearrange("(j p) t -> p j t", p=128)
        for g in range(KF // G):
            n0 = g * G * 128
            wgt, bwg, cwg = wg.next()
            wut, bwu, cwu = wu.next()
            self.wload(ph, wst, wgt[:], bwg, Wg[:, :, n0:n0 + G * 128], KD, G * 128)
            self.wload(ph, wst, wut[:], bwu, Wu[:, :, n0:n0 + G * 128], KD, G * 128)
            for s in range(len(self.T)):
                for (t0, w) in tiles_of(self.T[s], 510):
                    xt, bx, cx = xin.next()
                    g0 = self.goff[s] + t0 - 1
                    with self.nc.allow_non_contiguous_dma("feature-major scratch load"):
                        ph.dma("sp", xt[:, :, 0:w + 2], xv[:, :, g0:g0 + w + 2], cx, writes=[bx])
                    hot, bho, cho = ho.next()
                    for j in range(G):
                        jc = g * G + j
                        pg, bpg, _ = psg.next()
                        pu, bpu, _ = psu.next()
                        ph.mm([lambda e, k=k, pg=pg, wgt=wgt, xt=xt, j=j, w=w: e.matmul(
                            pg[:, 0:w + 2], lhsT=wgt[:, k, j * 128:(j + 1) * 128], rhs=xt[:, k, 0:w + 2],
                            start=(k == 0), stop=(k == KD - 1)) for k in range(KD)],
                            reads=[bwg, bx], writes=[bpg])
                        ph.mm([lambda e, k=k, pu=pu, wut=wut, xt=xt, j=j, w=w: e.matmul(
                            pu[:, 0:w], lhsT=wut[:, k, j * 128:(j + 1) * 128], rhs=xt[:, k, 1:w + 1],
                            start=(k == 0), stop=(k == KD - 1)) for k in range(KD)],
                            reads=[bwu, bx], writes=[bpu])
                        gst, bgs, _ = gs.next()
                        ph.op("act", lambda e, gst=gst, pg=pg, w=w: e.copy(out=gst[:, 0:w + 2], in_=pg[:, 0:w + 2]),
                              reads=[bpg], writes=[bgs])
                        tt, bt, _ = t1.next()
                        ph.op("dve", lambda e, tt=tt, gst=gst, jc=jc, w=w: e.tensor_scalar(
                            out=tt[:, 0:w], in0=gst[:, 0:w], scalar1=cw[:, 0, jc:jc + 1], scalar2=cb[:, 0, jc:jc + 1],
                            op0=ALU.mult, op1=ALU.add), reads=[bgs, bcw, bcb], writes=[bt])
                        ph.op("dve", lambda e, tt=tt, gst=gst, jc=jc, w=w: e.scalar_tensor_tensor(
                            out=tt[:, 0:w], in0=gst[:, 1:w + 1], scalar=cw[:, 1, jc:jc + 1], in1=tt[:, 0:w],
                            op0=ALU.mult, op1=ALU.add), reads=[bgs, bcw, bt], writes=[bt])
                        ph.op("dve", lambda e, tt=tt, gst=gst, jc=jc, w=w: e.scalar_tensor_tensor(
                            out=tt[:, 0:w], in0=gst[:, 2:w + 2], scalar=cw[:, 2, jc:jc + 1], in1=tt[:, 0:w],
                            op0=ALU.mult, op1=ALU.add), reads=[bgs, bcw, bt], writes=[bt])
                        sgt, bsg, _ = sg.next()
                        ph.op("act", lambda e, sgt=sgt, tt=tt, w=w: e.activation(
                            out=sgt[:, 0:w], in_=tt[:, 0:w], func=AF.Silu), reads=[bt], writes=[bsg])
                        ph.op("dve", lambda e, hot=hot, j=j, sgt=sgt, pu=pu, w=w: e.tensor_tensor(
                            out=hot[:, j, 0:w], in0=sgt[:, 0:w], in1=pu[:, 0:w], op=ALU.mult),
                            reads=[bsg, bpu], writes=[bho])
                    c0 = self.toff[s] + t0
                    with self.nc.allow_non_contiguous_dma("feature-major scratch store"):
                        ph.dma("sp", Hv[:, g * G:(g + 1) * G, c0:c0 + w], hot[:, :, 0:w], cho, reads=[bho])
        ph.finish()

    def phase_proj_res(self, name, W, nk, srcT, src_off=None):
        ph = Phase(self, name)
        G = 4
        TW = 512
        wd = Rot(ph, "wd", [128, nk, G * 128], BF16, 1, dma=False)
        wst = self.wstage(ph, 2)
        hin = Rot(ph, "hin", [128, nk, TW], BF16, 2)
        ps = Rot(ph, "ps", [128, 512], F32, 4, dma=False, psum=True)
        ho = Rot(ph, "ho", [128, G, TW], F32, 2)
        Wv = W.rearrange("(k p) n -> p k n", p=128)
        Sv = srcT.rearrange("(k p) t -> p k t", p=128)
        hv = self.hT.rearrange("(k p) t -> p k t", p=128)
        for g in range(KD // G):
            wt, bwt, cwt = wd.next()
            self.wload(ph, wst, wt[:], bwt, Wv[:, :, g * G * 128:(g + 1) * G * 128], nk, G * 128)
            for s in range(len(self.T)):
                for (t0, w) in tiles_of(self.T[s], TW):
                    c0 = self.toff[s] + t0
                    cs = c0 if src_off is None else src_off[s] + t0
                    xt, bx, cx = hin.next()
                    ph.dma("sp", xt[:, :, 0:w], Sv[:, :, cs:cs + w], cx, writes=[bx])
                    hot, bho, cho = ho.next()
                    with self.nc.allow_non_contiguous_dma("feature-major scratch load"):
                        ph.dma("sp", hot[:, :, 0:w], hv[:, g * G:(g + 1) * G, c0:c0 + w], cho, writes=[bho])
                    for j in range(G):
                        pt, bp, _ = ps.next()
                        ph.mm([lambda e, k=k, pt=pt, wt=wt, xt=xt, j=j, w=w: e.matmul(
                            pt[:, 0:w], lhsT=wt[:, k, j * 128:(j + 1) * 128], rhs=xt[:, k, 0:w],
                            start=(k == 0), stop=(k == nk - 1)) for k in range(nk)],
                            reads=[bwt, bx], writes=[bp])
                        ph.op("dve", lambda e, hot=hot, j=j, pt=pt, w=w: e.tensor_tensor(
                            out=hot[:, j, 0:w], in0=hot[:, j, 0:w], in1=pt[:, 0:w], op=ALU.add),
                            reads=[bp, bho], writes=[bho])
                    with self.nc.allow_non_contiguous_dma("feature-major scratch store"):
                        ph.dma("sp", hv[:, g * G:(g + 1) * G, c0:c0 + w], hot[:, :, 0:w], cho, reads=[bho])
        ph.finish()


    def phase_rg_in(self, j):
        ph = Phase(self, f"ra{j}")
        G = 4
        cw, bcw = self.load_cols(ph, "cw", self.inp["rg_conv_w"][j], KD)
        cb, bcb = self.load_cols(ph, "cb", self.inp["rg_conv_b"][j:j + 1], KD)
        wg = Rot(ph, "wg", [128, KD, G * 128], BF16, 2, dma=False)
        wst = self.wstage(ph)
        xin = Rot(ph, "xin", [128, KD, 512], BF16, 2)
        psg = Rot(ph, "psg", [128, 512], F32, 4, dma=False, psum=True)
        gs = Rot(ph, "gs", [128, 512], F32, 2, dma=False)
        t1 = Rot(ph, "t1", [128, 512], F32, 2, dma=False)
        sg = Rot(ph, "sg", [128, 512], F32, 2, dma=False)
        ho = Rot(ph, "ho", [128, G, 512], BF16, 2)
        fo = Rot(ph, "fo", [128, G, 512], F32, 2)
        xv = self.xnT.rearrange("(k p) t -> p k t", p=128)
        Wv = self.inp["rg_w_in"][j].rearrange("(k p) n -> p k n", p=128)
        gv = self.gT.rearrange("(j p) t -> p j t", p=128)
        xcv = self.XC.rearrange("(j p) t -> p j t", p=128)
        xbv = self.xcbT.rearrange("(j p) t -> p j t", p=128)
        for g in range(2 * KD // G):
            n0 = g * G * 128
            rec = g >= KD // G
            if (RG_IN_PARTS == "gate" and rec) or (RG_IN_PARTS == "rec" and not rec):
                continue
            wgt, bwg, cwg = wg.next()
            self.wload(ph, wst, wgt[:], bwg, Wv[:, :, n0:n0 + G * 128], KD, G * 128)
            for s in range(len(self.T)):
                for (t0, w) in tiles_of(self.T[s], 509):
                    xt, bx, cx = xin.next()
                    g0 = self.goff[s] + t0 - 1
                    ph.dma("sp", xt[:, :, 0:w + 3], xv[:, :, g0:g0 + w + 3], cx, writes=[bx])
                    hot, bho, cho = ho.next()
                    if rec:
                        fot, bfo, cfo = fo.next()
                    for jj in range(G):
                        jc = (g * G + jj) % KD
                        pg, bpg, _ = psg.next()
                        ph.mm([lambda e, k=k, pg=pg, wgt=wgt, xt=xt, jj=jj, w=w: e.matmul(
                            pg[:, 0:w + 3], lhsT=wgt[:, k, jj * 128:(jj + 1) * 128], rhs=xt[:, k, 0:w + 3],
                            start=(k == 0), stop=(k == KD - 1)) for k in range(KD)],
                            reads=[bwg, bx], writes=[bpg])
                        if not rec:
                            sqt, bsq, _ = gs.next()
                            ph.op("act", lambda e, sqt=sqt, pg=pg, w=w: e.activation(
                                out=sqt[:, 0:w], in_=pg[:, 1:w + 1], func=AF.Square), reads=[bpg], writes=[bsq])
                            ph.op("dve", lambda e, sqt=sqt, w=w: e.tensor_scalar(
                                out=sqt[:, 0:w], in0=sqt[:, 0:w], scalar1=0.044715, scalar2=1.0,
                                op0=ALU.mult, op1=ALU.add), reads=[bsq], writes=[bsq])
                            ph.op("dve", lambda e, sqt=sqt, pg=pg, w=w: e.tensor_tensor(
                                out=sqt[:, 0:w], in0=sqt[:, 0:w], in1=pg[:, 1:w + 1], op=ALU.mult),
                                reads=[bsq, bpg], writes=[bsq])
                            sgt, bsg, _ = sg.next()
                            ph.op("act", lambda e, sgt=sgt, sqt=sqt, w=w: e.activation(
                                out=sgt[:, 0:w], in_=sqt[:, 0:w], func=AF.Sigmoid, scale=1.5957691216),
                                reads=[bsq], writes=[bsg])
                            ph.op("dve", lambda e, hot=hot, jj=jj, sgt=sgt, pg=pg, w=w: e.tensor_tensor(
                                out=hot[:, jj, 0:w], in0=sgt[:, 0:w], in1=pg[:, 1:w + 1], op=ALU.mult),
                                reads=[bsg, bpg], writes=[bho])
                        else:
                            gst, bgs, _ = gs.next()
                            ph.op("act", lambda e, gst=gst, pg=pg, w=w: e.copy(out=gst[:, 0:w + 3], in_=pg[:, 0:w + 3]),
                                  reads=[bpg], writes=[bgs])
                            ph.op("dve", lambda e, fot=fot, jj=jj, gst=gst, jc=jc, w=w: e.tensor_scalar(
                                out=fot[:, jj, 0:w], in0=gst[:, 0:w], scalar1=cw[:, 0, jc:jc + 1],
                                scalar2=cb[:, 0, jc:jc + 1], op0=ALU.mult, op1=ALU.add),
                                reads=[bgs, bcw, bcb], writes=[bfo])
                            for tap in range(1, 4):
                                ph.op("dve", lambda e, fot=fot, jj=jj, gst=gst, jc=jc, w=w, tap=tap: e.scalar_tensor_tensor(
                                    out=fot[:, jj, 0:w], in0=gst[:, tap:tap + w], scalar=cw[:, tap, jc:jc + 1],
                                    in1=fot[:, jj, 0:w], op0=ALU.mult, op1=ALU.add),
                                    reads=[bgs, bcw, bfo], writes=[bfo])
                            ph.op("act", lambda e, hot=hot, fot=fot, jj=jj, w=w: e.copy(
                                out=hot[:, jj, 0:w], in_=fot[:, jj, 0:w]), reads=[bfo], writes=[bho])
                    c0 = self.toff[s] + t0
                    gg = g % (KD // G)
                    if not rec:
                        ph.dma("sp", gv[:, gg * G:(gg + 1) * G, c0:c0 + w], hot[:, :, 0:w], cho, reads=[bho])
                    else:
                        ph.dma("sp", xbv[:, gg * G:(gg + 1) * G, c0:c0 + w], hot[:, :, 0:w], cho, reads=[bho])
                        ph.dma("sp", xcv[:, gg * G:(gg + 1) * G, c0:c0 + w], fot[:, :, 0:w], cfo, reads=[bfo])
        ph.finish()

    def phase_rg_scan(self, j):
        ph = Phase(self, f"rs{j}")
        TM = max(self.T)
        ba, bba = self.load_cols(ph, "ba", self.inp["rg_b_a"][j], KD)
        bi, bbi = self.load_cols(ph, "bi", self.inp["rg_b_i"][j], KD)
        lam, blam = self.load_cols(ph, "lam", self.inp["rg_lam"][j], KD)
        ph.op("act", lambda e: e.activation(out=lam[:], in_=lam[:], func=AF.Exp, scale=-1.0),
              reads=[blam], writes=[blam])
        ph.op("act", lambda e: e.activation(out=lam[:], in_=lam[:], func=AF.Ln, bias=self._one[:, 0:1]),
              reads=[blam], writes=[blam])
        ph.op("dve", lambda e: e.tensor_scalar(out=lam[:], in0=lam[:], scalar1=-8.0, scalar2=None, op0=ALU.mult),
              reads=[blam], writes=[blam])
        wa = Rot(ph, "wa", [128, 2, 2, 256], BF16, 2, dma=False)
        wi = Rot(ph, "wi", [128, 2, 2, 256], BF16, 2, dma=False)
        wst = self.wstage(ph, 2)
        xc = Rot(ph, "xc", [128, 2, TM], F32, 1)
        xb = Rot(ph, "xb", [128, 2, TM], BF16, 1)
        gt = Rot(ph, "gt", [128, 2, TM], BF16, 1)
        At = Rot(ph, "A", [128, TM], F32, 1, dma=False)
        Ut = Rot(ph, "U", [128, TM], F32, 1, dma=False)
        Hf = Rot(ph, "Hf", [128, TM], F32, 1, dma=False)
        Hb = Rot(ph, "Hb", [128, TM], F32, 1, dma=False)
        mo = Rot(ph, "mo", [128, TM], BF16, 2)
        pa = Rot(ph, "pa", [128, 512], F32, 2, dma=False, psum=True)
        pi = Rot(ph, "pi", [128, 512], F32, 2, dma=False, psum=True)
        rt = Rot(ph, "rt", [128, 512], F32, 2, dma=False)
        it = Rot(ph, "it", [128, 512], F32, 2, dma=False)
        xcv = self.XC.rearrange("(j p) t -> p j t", p=128)
        xbv = self.xcbT.rearrange("(j p) t -> p j t", p=128)
        gv = self.gT.rearrange("(j p) t -> p j t", p=128)
        mv = self.mT.rearrange("(j p) t -> p j t", p=128)
        for b in range(8):
            wat, bwa, cwa = wa.next()
            wit, bwi, cwi = wi.next()
            for d in range(2):
                self.wload(ph, wst, wat[:, d], bwa, self.inp["rg_w_a"][j][d, b].rearrange("(k p) n -> p k n", p=128), 2, 256)
                self.wload(ph, wst, wit[:, d], bwi, self.inp["rg_w_i"][j][d, b].rearrange("(k p) n -> p k n", p=128), 2, 256)
            for s in range(len(self.T)):
                T = self.T[s]
                c0 = self.toff[s]
                xct, bxc, cxc = xc.next()
                xbt, bxb, cxb = xb.next()
                gtt, bgt, cgt = gt.next()
                ph.dma("sp", xct[:, :, 0:T], xcv[:, 2 * b:2 * b + 2, c0:c0 + T], cxc, writes=[bxc])
                ph.dma("sp", xbt[:, :, 0:T], xbv[:, 2 * b:2 * b + 2, c0:c0 + T], cxb, writes=[bxb])
                ph.dma("sp", gtt[:, :, 0:T], gv[:, 2 * b:2 * b + 2, c0:c0 + T], cgt, writes=[bgt])
                for n in range(2):
                    ch = 2 * b + n
                    hts = []
                    for d in range(2):
                        A, bA, _ = At.next()
                        U, bU, _ = Ut.next()
                        for (t0, w) in tiles_of(T, 512):
                            pat, bpa, _ = pa.next()
                            pit, bpi, _ = pi.next()
                            ph.mm([lambda e, k=k, pat=pat, wat=wat, xbt=xbt, d=d, n=n, t0=t0, w=w: e.matmul(
                                pat[:, 0:w], lhsT=wat[:, d, k, n * 128:(n + 1) * 128], rhs=xbt[:, k, t0:t0 + w],
                                start=(k == 0), stop=(k == 1)) for k in range(2)], reads=[bwa, bxb], writes=[bpa])
                            ph.mm([lambda e, k=k, pit=pit, wit=wit, xbt=xbt, d=d, n=n, t0=t0, w=w: e.matmul(
                                pit[:, 0:w], lhsT=wit[:, d, k, n * 128:(n + 1) * 128], rhs=xbt[:, k, t0:t0 + w],
                                start=(k == 0), stop=(k == 1)) for k in range(2)], reads=[bwi, bxb], writes=[bpi])
                            r, br, _ = rt.next()
                            ii, bii, _ = it.next()
                            ph.op("act", lambda e, r=r, pat=pat, d=d, ch=ch, w=w: e.activation(
                                out=r[:, 0:w], in_=pat[:, 0:w], func=AF.Sigmoid, bias=ba[:, d, ch:ch + 1]),
                                reads=[bpa, bba], writes=[br])
                            ph.op("act", lambda e, ii=ii, pit=pit, d=d, ch=ch, w=w: e.activation(
                                out=ii[:, 0:w], in_=pit[:, 0:w], func=AF.Sigmoid, bias=bi[:, d, ch:ch + 1]),
                                reads=[bpi, bbi], writes=[bii])
                            ph.op("act", lambda e, A=A, r=r, d=d, ch=ch, t0=t0, w=w: e.activation(
                                out=A[:, t0:t0 + w], in_=r[:, 0:w], func=AF.Exp, scale=lam[:, d, ch:ch + 1]),
                                reads=[br, blam], writes=[bA])
                            ph.op("dve", lambda e, A=A, r=r, t0=t0, w=w: e.tensor_tensor(
                                out=r[:, 0:w], in0=A[:, t0:t0 + w], in1=A[:, t0:t0 + w], op=ALU.mult),
                                reads=[bA, br], writes=[br])
                            ph.op("dve", lambda e, r=r, w=w: e.tensor_scalar(
                                out=r[:, 0:w], in0=r[:, 0:w], scalar1=-1.0, scalar2=1.0, op0=ALU.mult, op1=ALU.add),
                                reads=[br], writes=[br])
                            ph.op("dve", lambda e, r=r, w=w: e.tensor_scalar(
                                out=r[:, 0:w], in0=r[:, 0:w], scalar1=0.0, scalar2=None, op0=ALU.max),
                                reads=[br], writes=[br])
                            ph.op("act", lambda e, r=r, w=w: e.activation(out=r[:, 0:w], in_=r[:, 0:w], func=AF.Sqrt),
                                  reads=[br], writes=[br])
                            ph.op("dve", lambda e, r=r, ii=ii, w=w: e.tensor_tensor(
                                out=r[:, 0:w], in0=r[:, 0:w], in1=ii[:, 0:w], op=ALU.mult),
                                reads=[br, bii], writes=[br])
                            ph.op("dve", lambda e, U=U, r=r, xct=xct, n=n, t0=t0, w=w: e.tensor_tensor(
                                out=U[:, t0:t0 + w], in0=r[:, 0:w], in1=xct[:, n, t0:t0 + w], op=ALU.mult),
                                reads=[br, bxc], writes=[bU])
                        H, bH, _ = (Hf if d == 0 else Hb).next()
                        if d == 0:
                            ph.op("dve", lambda e, H=H, A=A, U=U, T=T: e.tensor_tensor_scan(
                                out=H[:, 0:T], data0=A[:, 0:T], data1=U[:, 0:T], initial=0.0,
                                op0=ALU.mult, op1=ALU.add), reads=[bA, bU], writes=[bH])
                        else:
                            ph.op("dve", lambda e, H=H, A=A, U=U, T=T: e.tensor_tensor_scan(
                                out=H[:, 0:T][:, ::-1], data0=A[:, 0:T][:, ::-1], data1=U[:, 0:T][:, ::-1], initial=0.0,
                                op0=ALU.mult, op1=ALU.add), reads=[bA, bU], writes=[bH])
                        hts.append((H, bH))
                    (H0, bH0), (H1, bH1) = hts
                    ph.op("pool", lambda e, H0=H0, H1=H1, T=T: e.tensor_tensor(
                        out=H0[:, 0:T], in0=H0[:, 0:T], in1=H1[:, 0:T], op=ALU.add), reads=[bH0, bH1], writes=[bH0])
                    mt, bm, cm = mo.next()
                    ph.op("pool", lambda e, mt=mt, H0=H0, gtt=gtt, n=n, T=T: e.tensor_tensor(
                        out=mt[:, 0:T], in0=H0[:, 0:T], in1=gtt[:, n, 0:T], op=ALU.mult),
                        reads=[bH0, bgt], writes=[bm])
                    ph.dma("sp", mv[:, ch, c0:c0 + T], mt[:, 0:T], cm, reads=[bm])
        ph.finish()


    def mixer_na(self, j):
        self.phase_na_qkv(j)
        self.phase_na_attn(j)
        self.phase_proj_res(f"no{j}", self.inp["na_w_o"][j], KD, self.mT)

    def phase_na_qkv(self, j):
        ph = Phase(self, f"nq{j}")
        G = 4
        wg = Rot(ph, "wg", [128, KD, G * 128], BF16, 2, dma=False)
        wst = self.wstage(ph)
        xin = Rot(ph, "xin", [128, KD, 512], BF16, 2)
        psg = Rot(ph, "psg", [128, 512], F32, 4, dma=False, psum=True)
        ho = Rot(ph, "ho", [128, G, 512], BF16, 2)
        vo = Rot(ph, "vo", [128, 512], BF16, 2)
        xv = self.xnT.rearrange("(k p) t -> p k t", p=128)
        Wv = self.inp["na_w_qkv"][j].rearrange("(k p) n -> p k n", p=128)
        qv = self.gT.rearrange("(j p) t -> p j t", p=128)
        kv = self.xcbT.rearrange("(j p) t -> p j t", p=128)
        z = ph.sb("z", [120, 320], F32)
        bz = Buf("z")
        cz = ph.dctr()
        bpad = Buf("pad")
        ph.op("pool", lambda e: e.memset(z[:], 0.0), writes=[bz])
        ph.dma("sp", self.rpbpad.rearrange("(a b) c -> a (b c)", b=2), z[:], cz, reads=[bz], writes=[bpad])
        ph.dma("sp", self.rpbpad[:, 64:95], self.inp["na_rpb"][j].rearrange("h r c -> (h r) c"), cz, writes=[bpad])
        for g in range(3 * KD // G):
            n0 = g * G * 128
            wgt, bwg, cwg = wg.next()
            self.wload(ph, wst, wgt[:], bwg, Wv[:, :, n0:n0 + G * 128], KD, G * 128)
            which = g // (KD // G)
            gg = g % (KD // G)
            for s in range(len(self.T)):
                if which < 2:
                    for (t0, w) in tiles_of(self.T[s], 512):
                        xt, bx, cx = xin.next()
                        g0 = self.goff[s] + t0
                        ph.dma("sp", xt[:, :, 0:w], xv[:, :, g0:g0 + w], cx, writes=[bx])
                        hot, bho, cho = ho.next()
                        for jj in range(G):
                            pg, bpg, _ = psg.next()
                            ph.mm([lambda e, k=k, pg=pg, wgt=wgt, xt=xt, jj=jj, w=w: e.matmul(
                                pg[:, 0:w], lhsT=wgt[:, k, jj * 128:(jj + 1) * 128], rhs=xt[:, k, 0:w],
                                start=(k == 0), stop=(k == KD - 1)) for k in range(KD)],
                                reads=[bwg, bx], writes=[bpg])
                            sc = 128 ** -0.5 if which == 0 else 1.0
                            if jj % 2 == 0:
                                ph.op("act", lambda e, hot=hot, jj=jj, pg=pg, w=w, sc=sc: e.activation(
                                    out=hot[:, jj, 0:w], in_=pg[:, 0:w], func=AF.Copy, scale=sc),
                                    reads=[bpg], writes=[bho])
                            else:
                                ph.op("dve", lambda e, hot=hot, jj=jj, pg=pg, w=w, sc=sc: e.tensor_scalar(
                                    out=hot[:, jj, 0:w], in0=pg[:, 0:w], scalar1=sc, scalar2=None, op0=ALU.mult),
                                    reads=[bpg], writes=[bho])
                        c0 = self.toff[s] + t0
                        dst = qv if which == 0 else kv
                        ph.dma("sp", dst[:, gg * G:(gg + 1) * G, c0:c0 + w], hot[:, :, 0:w], cho, reads=[bho])
                else:
                    for (t0, w) in tiles_of(self.T[s], 128):
                        xt, bx, cx = xin.next()
                        g0 = self.goff[s] + t0
                        ph.dma("sp", xt[:, :, 0:w], xv[:, :, g0:g0 + w], cx, writes=[bx])
                        pg, bpg, _ = psg.next()
                        ph.mm([lambda e, k=k, pg=pg, wgt=wgt, xt=xt, w=w: e.matmul(
                            pg[0:w, :], lhsT=xt[:, k, 0:w], rhs=wgt[:, k, :],
                            start=(k == 0), stop=(k == KD - 1)) for k in range(KD)],
                            reads=[bwg, bx], writes=[bpg])
                        vt, bvt, cvt = vo.next()
                        ph.op("act", lambda e, vt=vt, pg=pg, w=w: e.copy(out=vt[0:w, :], in_=pg[0:w, :]),
                              reads=[bpg], writes=[bvt])
                        c0 = self.toff[s] + t0
                        ph.dma("sp", self.Vtm[c0:c0 + w, gg * 512:(gg + 1) * 512], vt[0:w, :], cvt, reads=[bvt])
        ph.finish()

    def phase_na_attn(self, j):
        ph = Phase(self, f"na{j}")
        TM = max(self.T)
        RM = (TM - N_META) // GRID_W
        NH = 16
        ones = ph.sb("ones", [128, 128], BF16)
        b_on = Buf("ones")
        ph.op("pool", lambda e: e.memset(ones[:], 1.0), writes=[b_on])
        E = ph.sb("E", [64, NH * 15, 64], F32)
        msk = ph.sb("msk", [64, 64], F32)
        bE, bmsk = Buf("E"), Buf("msk")
        c1 = ph.dctr()
        pad_t = self.rpbpad.tensor
        ph.dma("sp", msk[:], self.inp["c_namask"], c1, writes=[bmsk])
        btr = Rot(ph, "Bt", [64, 15, 64], F32, 2)
        for h in range(NH):
            Bt, bBt, cBt = btr.next()
            src = bass.AP(pad_t, 16 + 160 * 15 * h, [[1, 64], [160, 15], [1, 64]])
            ph.dma("sp", Bt[:], src, cBt, writes=[bBt])
            ph.op("act", lambda e, Bt=Bt: e.activation(out=Bt[:], in_=Bt[:], func=AF.Exp), reads=[bBt], writes=[bBt])
            for ro in range(15):
                hr = h * 15 + ro
                ph.op("dve" if hr % 2 == 0 else "pool", lambda e, hr=hr, ro=ro, Bt=Bt: e.tensor_tensor(
                    out=E[:, hr, :], in0=Bt[:, ro, ::-1], in1=msk[:], op=ALU.mult), reads=[bBt, bmsk], writes=[bE])
        mbT = ph.sb("mbT", [16, NH], F32)
        bmb = Buf("mbT")
        ph.dma("sp", mbT[:], self.inp["na_meta_bias"][j].rearrange("h m -> m h"), ph.dctr(), writes=[bmb])
        qh = Rot(ph, "qh", [128, TM], BF16, 2)
        kh = Rot(ph, "kh", [128, TM], BF16, 2)
        vg = Rot(ph, "vg", [64, RM, 128], BF16, 2)
        vm = Rot(ph, "vm", [16, 128], BF16, 2)
        oh = Rot(ph, "oh", [128, TM], BF16, 2)
        pss = Rot(ph, "pss", [64, 8, 64], F32, 2, dma=False, psum=True)
        psm = Rot(ph, "psm", [16, 64], F32, 2, dma=False, psum=True)
        pso = Rot(ph, "pso", [128, 2, 64], F32, 2, dma=False, psum=True)
        pe_ = Rot(ph, "pe", [64, 8, 64], F32, 2, dma=False)
        pb = Rot(ph, "pb", [64, 8, 64], BF16, 2, dma=False)
        pm = Rot(ph, "pm", [16, 64], BF16, 2, dma=False)
        rd = Rot(ph, "rd", [128, 64], F32, 2, dma=False)
        qv = self.gT.rearrange("(j p) t -> p j t", p=128)
        kv = self.xcbT.rearrange("(j p) t -> p j t", p=128)
        mv = self.mT.rearrange("(j p) t -> p j t", p=128)
        for s in range(len(self.T)):
            T = self.T[s]
            rows = (T - N_META) // GRID_W
            c0 = self.toff[s]
            for h in range(NH):
                qt, bq, cq = qh.next()
                kt, bk, ck = kh.next()
                vgt, bvg, cvg = vg.next()
                vmt, bvm, cvm = vm.next()
                ot, bo, co = oh.next()
                ph.dma("sp", qt[:, 0:T], qv[:, h, c0:c0 + T], cq, writes=[bq])
                ph.dma("sp", kt[:, 0:T], kv[:, h, c0:c0 + T], ck, writes=[bk])
                ph.dma("sp", vgt[:, 0:rows, :], self.Vtm[c0 + N_META:c0 + T, h * 128:(h + 1) * 128].rearrange(
                    "(r p) d -> p r d", p=64), cvg, writes=[bvg])
                ph.dma("sp", vmt[:], self.Vtm[c0:c0 + N_META, h * 128:(h + 1) * 128], cvm, writes=[bvm])
                for r in range(-1, rows):
                    meta_q = r < 0
                    q0 = 0 if meta_q else N_META + GRID_W * r
                    nq = N_META if meta_q else GRID_W
                    rs_ = 0 if meta_q else int(np.clip(r - 4, 0, rows - 8))
                    ro0 = rs_ - r + 7
                    pmt_ps, bpm_ps, _ = psm.next()
                    ph.mm([lambda e, pmt_ps=pmt_ps, kt=kt, qt=qt, q0=q0, nq=nq: e.matmul(
                        pmt_ps[:, 0:nq], lhsT=kt[:, 0:N_META], rhs=qt[:, q0:q0 + nq], start=True, stop=True)],
                        reads=[bk, bq], writes=[bpm_ps])
                    pmt, bpmt, _ = pm.next()
                    ph.op("act", lambda e, pmt=pmt, pmt_ps=pmt_ps, h=h, nq=nq: e.activation(
                        out=pmt[:, 0:nq], in_=pmt_ps[:, 0:nq], func=AF.Exp, bias=mbT[:, h:h + 1]),
                        reads=[bpm_ps, bmb], writes=[bpmt])
                    if not meta_q:
                        pst, bps, _ = pss.next()
                        ph.mm([lambda e, kk=kk, pst=pst, kt=kt, qt=qt, q0=q0, rs_=rs_: e.matmul(
                            pst[:, kk, :], lhsT=kt[:, N_META + GRID_W * (rs_ + kk):N_META + GRID_W * (rs_ + kk + 1)],
                            rhs=qt[:, q0:q0 + GRID_W], start=True, stop=True) for kk in range(8)],
                            reads=[bk, bq], writes=[bps])
                        pet, bpe, _ = pe_.next()
                        ph.op("act", lambda e, pet=pet, pst=pst: e.activation(out=pet[:], in_=pst[:], func=AF.Exp),
                              reads=[bps], writes=[bpe])
                        pbt, bpb, _ = pb.next()
                        ph.op("dve", lambda e, pbt=pbt, pet=pet, h=h, ro0=ro0: e.tensor_tensor(
                            out=pbt[:], in0=pet[:], in1=E[:, h * 15 + ro0:h * 15 + ro0 + 8, :], op=ALU.mult),
                            reads=[bpe, bE], writes=[bpb])
                    pot, bpo, _ = pso.next()
                    fns = []
                    rdl = [bvm, bpmt, b_on]
                    if not meta_q:
                        rdl += [bvg, bpb]
                        for kk in range(8):
                            fns.append(lambda e, kk=kk, pot=pot, vgt=vgt, pbt=pbt, rs_=rs_: e.matmul(
                                pot[:, 0, 0:GRID_W], lhsT=vgt[:, rs_ + kk, :], rhs=pbt[:, kk, :],
                                start=(kk == 0), stop=False))
                    fns.append(lambda e, pot=pot, vmt=vmt, pmt=pmt, nq=nq, meta_q=meta_q: e.matmul(
                        pot[:, 0, 0:nq], lhsT=vmt[:], rhs=pmt[:, 0:nq], start=meta_q, stop=True))
                    if not meta_q:
                        for kk in range(8):
                            fns.append(lambda e, kk=kk, pot=pot, pbt=pbt: e.matmul(
                                pot[:, 1, 0:GRID_W], lhsT=ones[0:64, :], rhs=pbt[:, kk, :],
                                start=(kk == 0), stop=False))
                    fns.append(lambda e, pot=pot, pmt=pmt, nq=nq, meta_q=meta_q: e.matmul(
                        pot[:, 1, 0:nq], lhsT=ones[0:16, :], rhs=pmt[:, 0:nq], start=meta_q, stop=True))
                    ph.mm(fns, reads=rdl, writes=[bpo])
                    rdt, brd, _ = rd.next()
                    ph.op("dve", lambda e, rdt=rdt, pot=pot, nq=nq: e.reciprocal(out=rdt[:, 0:nq], in_=pot[:, 1, 0:nq]),
                          reads=[bpo], writes=[brd])
                    ph.op("dve", lambda e, ot=ot, rdt=rdt, pot=pot, q0=q0, nq=nq: e.tensor_tensor(
                        out=ot[:, q0:q0 + nq], in0=pot[:, 0, 0:nq], in1=rdt[:, 0:nq], op=ALU.mult),
                        reads=[bpo, brd], writes=[bo])
                ph.dma("sp", mv[:, h, c0:c0 + T], ot[:, 0:T], co, reads=[bo])
        ph.finish()


    def gdn_layout(self):
        self.Tc = [((t + 127) // 128) * 128 for t in self.T]
        self.coff = np.concatenate([[0], np.cumsum(self.Tc)]).astype(int).tolist()
        self.TTC = self.coff[-1]
        nc = self.nc
        if not hasattr(self, "GqT_"):
            kd = "ExternalOutput" if DEBUG_GDN else "Internal"
            self.GqT_ = nc.dram_tensor("s_GqT", [D, self.TTC], BF16, kind=kd).ap()
            self.GkT_ = nc.dram_tensor("s_GkT", [D, self.TTC], BF16, kind=kd).ap()
            self.GvT_ = nc.dram_tensor("s_GvT", [2 * D, self.TTC], BF16, kind=kd).ap()
            self.GzT_ = nc.dram_tensor("s_GzT", [2 * D, self.TTC], BF16, kind=kd).ap()
            self.Gbg_ = nc.dram_tensor("s_Gbg", [128, self.TTC], F32, kind=kd).ap()
            self.GoT_ = nc.dram_tensor("s_GoT", [2 * D, self.TTC], BF16, kind=kd).ap()
            self.Otm_ = nc.dram_tensor("s_Otm", [self.TTC, 2 * D], F32, kind=kd).ap()

    def mixer_gdn(self, j):
        self.gdn_layout()
        self.phase_gdn_in(j)
        self.phase_gdn_cumsum()
        self.phase_gdn_core(j, 0)
        self.phase_gdn_core(j, 1)
        self.phase_proj_res(f"go{j}", self.inp["gdn_w_out"][j], 32, self.GoT_, src_off=self.coff)

    def phase_gdn_in(self, j):
        ph = Phase(self, f"ga{j}")
        G = 4
        cw, bcw = self.load_cols(ph, "cw", self.inp["gdn_conv_w"][j], 64)
        ones = ph.sb("ones", [128, 128], F32)
        b_on = Buf("ones")
        ph.op("pool", lambda e: e.memset(ones[:], 1.0), writes=[b_on])
        prm = ph.sb("prm", [128, 2], F32)
        bprm = Buf("prm")
        cp = ph.dctr()
        ph.op("pool", lambda e: e.memset(prm[:], 0.0), writes=[bprm])
        ph.dma("sp", prm[64:128, 0:1], self.inp["gdn_dt_bias"][j].rearrange("d (h o) -> (d h) o", o=1), cp, writes=[bprm])
        ph.dma("sp", prm[64:128, 1:2], self.inp["gdn_a_log"][j].rearrange("d (h o) -> (d h) o", o=1), cp, writes=[bprm])
        ph.op("act", lambda e: e.activation(out=prm[64:128, 1:2], in_=prm[64:128, 1:2], func=AF.Exp),
              reads=[bprm], writes=[bprm])
        ph.op("dve", lambda e: e.tensor_scalar(out=prm[64:128, 1:2], in0=prm[64:128, 1:2], scalar1=-1.0, scalar2=None,
                                               op0=ALU.mult), reads=[bprm], writes=[bprm])
        zb = ph.sb("zb", [128, 32, 128], BF16)
        zf = ph.sb("zf", [128, 128], F32)
        bzb, bzf = Buf("zb"), Buf("zf")
        cz = ph.dctr()
        ph.op("pool", lambda e: e.memset(zb[:], 0.0), writes=[bzb])
        ph.op("pool", lambda e: e.memset(zf[:], 0.0), writes=[bzf])
        for s in range(len(self.T)):
            npad = self.Tc[s] - self.T[s]
            if npad == 0:
                continue
            p0 = self.coff[s] + self.T[s]
            for tns, nch in ((self.GqT_, 16), (self.GkT_, 16), (self.GvT_, 32), (self.GzT_, 32)):
                ph.dma("sp", tns.rearrange("(j p) t -> p j t", p=128)[:, :, p0:p0 + npad], zb[:, 0:nch, 0:npad], cz, reads=[bzb])
            ph.dma("sp", self.Gbg_[:, p0:p0 + npad], zf[:, 0:npad], cz, reads=[bzf])
        wg = Rot(ph, "wg", [128, KD, G * 128], BF16, 2, dma=False)
        wst = self.wstage(ph)
        xin = Rot(ph, "xin", [128, KD, 512], BF16, 2)
        psg = Rot(ph, "psg", [128, 512], F32, 4, dma=False, psum=True)
        pss = Rot(ph, "pss", [128, 512], F32, 2, dma=False, psum=True)
        gs = Rot(ph, "gs", [128, 512], F32, 2, dma=False)
        t1 = Rot(ph, "t1", [128, 512], F32, 2, dma=False)
        sq = Rot(ph, "sq", [128, 512], F32, 2, dma=False)
        rn = Rot(ph, "rn", [128, 512], F32, 2, dma=False)
        ho = Rot(ph, "ho", [128, G, 512], BF16, 2)
        fo = Rot(ph, "fo", [128, 512], F32, 2)
        xv = self.xnT.rearrange("(k p) t -> p k t", p=128)
        Wv = self.inp["gdn_w_in"][j].rearrange("(k p) n -> p k n", p=128)
        dsts = [self.GqT_.rearrange("(j p) t -> p j t", p=128), self.GkT_.rearrange("(j p) t -> p j t", p=128),
                self.GvT_.rearrange("(j p) t -> p j t", p=128), self.GzT_.rearrange("(j p) t -> p j t", p=128)]
        for g in range(25):
            n0 = g * G * 128
            ng = G if g < 24 else 1
            kindg = 0 if g < 4 else 1 if g < 8 else 2 if g < 16 else 3 if g < 24 else 4
            gbase = [0, 4, 8, 16, 24][kindg]
            wgt, bwg, cwg = wg.next()
            self.wload(ph, wst, wgt[:, :, 0:ng * 128], bwg, Wv[:, :, n0:n0 + ng * 128], KD, ng * 128)
            for s in range(len(self.T)):
                for (t0, w) in tiles_of(self.T[s], 509):
                    xt, bx, cx = xin.next()
                    g0 = self.goff[s] + t0 - 1
                    ph.dma("sp", xt[:, :, 0:w + 3], xv[:, :, g0:g0 + w + 3], cx, writes=[bx])
                    if kindg < 4:
                        hot, bho, cho = ho.next()
                    else:
                        fot, bfo, cfo = fo.next()
                    for jj in range(ng):
                        jc = g * G + jj
                        pg, bpg, _ = psg.next()
                        ph.mm([lambda e, k=k, pg=pg, wgt=wgt, xt=xt, jj=jj, w=w: e.matmul(
                            pg[:, 0:w + 3], lhsT=wgt[:, k, jj * 128:(jj + 1) * 128], rhs=xt[:, k, 0:w + 3],
                            start=(k == 0), stop=(k == KD - 1)) for k in range(KD)],
                            reads=[bwg, bx], writes=[bpg])
                        if kindg <= 2:
                            gst, bgs, _ = gs.next()
                            ph.op("act", lambda e, gst=gst, pg=pg, w=w: e.copy(out=gst[:, 0:w + 3], in_=pg[:, 0:w + 3]),
                                  reads=[bpg], writes=[bgs])
                            tt, bt, _ = t1.next()
                            ph.op("dve", lambda e, tt=tt, gst=gst, jc=jc, w=w: e.tensor_scalar(
                                out=tt[:, 0:w], in0=gst[:, 0:w], scalar1=cw[:, 0, jc:jc + 1], scalar2=None,
                                op0=ALU.mult), reads=[bgs, bcw], writes=[bt])
                            for tap in range(1, 4):
                                ph.op("dve", lambda e, tt=tt, gst=gst, jc=jc, w=w, tap=tap: e.scalar_tensor_tensor(
                                    out=tt[:, 0:w], in0=gst[:, tap:tap + w], scalar=cw[:, tap, jc:jc + 1],
                                    in1=tt[:, 0:w], op0=ALU.mult, op1=ALU.add), reads=[bgs, bcw, bt], writes=[bt])
                            if kindg == 2:
                                ph.op("act", lambda e, hot=hot, jj=jj, tt=tt, w=w: e.activation(
                                    out=hot[:, jj, 0:w], in_=tt[:, 0:w], func=AF.Silu), reads=[bt], writes=[bho])
                            else:
                                ph.op("act", lambda e, tt=tt, w=w: e.activation(
                                    out=tt[:, 0:w], in_=tt[:, 0:w], func=AF.Silu), reads=[bt], writes=[bt])
                                sqt, bsq, _ = sq.next()
                                ph.op("pool", lambda e, sqt=sqt, tt=tt, w=w: e.tensor_tensor(
                                    out=sqt[:, 0:w], in0=tt[:, 0:w], in1=tt[:, 0:w], op=ALU.mult),
                                    reads=[bt], writes=[bsq])
                                ps2, bps2, _ = pss.next()
                                ph.mm([lambda e, ps2=ps2, sqt=sqt, w=w: e.matmul(
                                    ps2[:, 0:w], lhsT=ones[:], rhs=sqt[:, 0:w], start=True, stop=True)],
                                    reads=[bsq, b_on], writes=[bps2])
                                rnt, brn, _ = rn.next()
                                ph.op("act", lambda e, rnt=rnt, ps2=ps2, w=w: e.activation(
                                    out=rnt[:, 0:w], in_=ps2[:, 0:w], func=AF.Sqrt, bias=self._eps[:, 0:1]),
                                    reads=[bps2], writes=[brn])
                                ph.op("dve", lambda e, rnt=rnt, w=w: e.reciprocal(out=rnt[:, 0:w], in_=rnt[:, 0:w]),
                                      reads=[brn], writes=[brn])
                                sc = 128 ** -0.5 if kindg == 0 else 1.0
                                ph.op("dve", lambda e, hot=hot, jj=jj, tt=tt, rnt=rnt, w=w, sc=sc: e.scalar_tensor_tensor(
                                    out=hot[:, jj, 0:w], in0=tt[:, 0:w], scalar=sc, in1=rnt[:, 0:w],
                                    op0=ALU.mult, op1=ALU.mult), reads=[bt, brn], writes=[bho])
                        elif kindg == 3:
                            ph.op("act", lambda e, hot=hot, jj=jj, pg=pg, w=w: e.activation(
                                out=hot[:, jj, 0:w], in_=pg[:, 1:w + 1], func=AF.Silu), reads=[bpg], writes=[bho])
                        else:
                            ph.op("act", lambda e, fot=fot, pg=pg, w=w: e.activation(
                                out=fot[0:64, 0:w], in_=pg[0:64, 1:w + 1], func=AF.Sigmoid), reads=[bpg], writes=[bfo])
                            ph.op("act", lambda e, fot=fot, pg=pg, w=w: e.activation(
                                out=fot[64:128, 0:w], in_=pg[64:128, 1:w + 1], func=AF.Exp, bias=prm[64:128, 0:1]),
                                reads=[bpg, bprm], writes=[bfo])
                            ph.op("act", lambda e, fot=fot, w=w: e.activation(
                                out=fot[64:128, 0:w], in_=fot[64:128, 0:w], func=AF.Ln, bias=self._one[64:128, 0:1]),
                                reads=[bfo], writes=[bfo])
                            ph.op("dve", lambda e, fot=fot, w=w: e.tensor_scalar(
                                out=fot[64:128, 0:w], in0=fot[64:128, 0:w], scalar1=prm[64:128, 1:2], scalar2=None,
                                op0=ALU.mult), reads=[bfo, bprm], writes=[bfo])
                    c0 = self.coff[s] + t0
                    if kindg < 4:
                        jb = g * G - gbase * G
                        ph.dma("sp", dsts[kindg][:, jb:jb + G, c0:c0 + w], hot[:, :, 0:w], cho, reads=[bho])
                    else:
                        ph.dma("sp", self.Gbg_[:, c0:c0 + w], fot[:, 0:w], cfo, reads=[bfo])
        ph.finish()

    def phase_gdn_cumsum(self):
        ph = Phase(self, "gc")
        TM = max(self.Tc)
        gt = Rot(ph, "g", [64, TM], F32, 2)
        Gt = Rot(ph, "G", [64, TM], F32, 2)
        one = ph.sb("one", [64, 128], F32)
        b1 = Buf("one")
        tot = ph.sb("tot", [64, 1], F32)
        btot = Buf("tot")
        ph.op("pool", lambda e: e.memset(one[:], 1.0), writes=[b1])
        for s in range(len(self.T)):
            Tc = self.Tc[s]
            c0 = self.coff[s]
            g, bg, cg = gt.next()
            Gs, bG, cG = Gt.next()
            ph.dma("sp", g[:, 0:Tc], self.Gbg_[64:128, c0:c0 + Tc], cg, writes=[bg])
            for c in range(Tc // 128):
                sl = slice(c * 128, (c + 1) * 128)
                ph.op("dve", lambda e, Gs=Gs, g=g, sl=sl: e.tensor_tensor_scan(
                    out=Gs[:, sl], data0=one[:, :], data1=g[:, sl], initial=0.0, op0=ALU.mult, op1=ALU.add),
                    reads=[bg, b1], writes=[bG])
                ph.op("dve", lambda e, Gs=Gs, g=g, sl=sl, c=c: e.scalar_tensor_tensor(
                    out=g[32:64, sl], in0=Gs[32:64, sl], scalar=-1.0, in1=g[32:64, sl], op0=ALU.mult, op1=ALU.add),
                    reads=[bG, bg], writes=[bg])
                ph.op("act", lambda e, Gs=Gs, c=c: e.copy(out=tot[32:64, 0:1], in_=Gs[32:64, c * 128 + 127:c * 128 + 128]),
                      reads=[bG], writes=[btot])
                ph.op("dve", lambda e, Gs=Gs, g=g, sl=sl, c=c: e.tensor_scalar(
                    out=Gs[32:64, sl], in0=g[32:64, sl], scalar1=tot[32:64, 0:1], scalar2=None,
                    op0=ALU.add), reads=[btot, bg], writes=[bG])
            ph.dma("sp", self.Gbg_[64:128, c0:c0 + Tc], Gs[:, 0:Tc], cG, reads=[bG], writes=[])
        ph.finish()


    def phase_gdn_core(self, j, d):
        ph = Phase(self, f"gd{j}{d}")
        nc = self.nc
        C = 128
        TTC = self.TTC

        def bc_last(ap2, n):
            return bass.AP(ap2.tensor, ap2.offset, [list(ap2.ap[0]), list(ap2.ap[1]), [0, n]])

        def bc_mid(ap2, n):
            return bass.AP(ap2.tensor, ap2.offset, [list(ap2.ap[0]), [0, n], list(ap2.ap[1])])

        def rep2(ap3):
            return bass.AP(ap3.tensor, ap3.offset, [list(ap3.ap[0]), list(ap3.ap[1]), [0, 2], list(ap3.ap[2])])

        def v4(t):
            return t[:].rearrange("p (a b) i -> p a b i", b=2)

        cc = ph.dctr()
        ident_f = ph.sb("idf", [128, 128], F32)
        ident_b = ph.sb("idb", [128, 128], BF16)
        b_idf, b_idb = Buf("idf"), Buf("idb")
        ph.dma("sp", ident_f[:], self.inp["c_ident"], ph.dctr(), writes=[b_idf])
        ph.op("dve", lambda e: e.tensor_copy(out=ident_b[:], in_=ident_f[:]), reads=[b_idf], writes=[b_idb])
        Lm = ph.sb("Lm", [128, 128], F32)
        Um = ph.sb("Um", [128, 128], F32)
        bL, bU = Buf("L"), Buf("U")
        ph.dma("sp", Lm[:], self.inp["c_tril"], ph.dctr(), writes=[bL])
        ph.dma("sp", Um[:], self.inp["c_triu"], ph.dctr(), writes=[bU])
        Mi, MTi = (Lm, Um) if d == 0 else (Um, Lm)
        Msn = ph.sb("Msn", [128, 128], F32)
        MTsn = ph.sb("MTsn", [128, 128], F32)
        bM = Buf("masks")
        ph.op("dve", lambda e: e.tensor_tensor(out=Msn[:], in0=ident_f[:], in1=Mi[:], op=ALU.subtract),
              reads=[b_idf, bL, bU], writes=[bM])
        ph.op("dve", lambda e: e.tensor_tensor(out=MTsn[:], in0=ident_f[:], in1=MTi[:], op=ALU.subtract),
              reads=[b_idf, bL, bU, bM], writes=[bM])
        if d == 1:
            nw = ph.sb("nw", [128, 128], F32)
            bnw = Buf("nw")
            nwsrc = self.inp["gdn_norm_w"][j]
            ph.dma("sp", nw[:], bass.AP(nwsrc.tensor, nwsrc.offset, [[0, 128], [1, 128]]), cc, writes=[bnw])
        Sf = ph.sb("Sf", [128, 32, 128], F32)
        Sb = ph.sb("Sb", [128, 32, 128], BF16)
        bSf, bSb = Buf("Sf"), Buf("Sb")
        kTc = Rot(ph, "kTc", [128, 16, C], BF16, 1)
        qTc = Rot(ph, "qTc", [128, 16, C], BF16, 1)
        vTc = Rot(ph, "vTc", [128, 32, C], BF16, 1)
        BGc = Rot(ph, "BGc", [128, C], F32, 1)
        ktm = ph.sb("ktm", [128, 16, 128], BF16)
        vtm = ph.sb("vtm", [128, 32, 128], BF16)
        cols = ph.sb("cols", [128, 128], F32)
        e1 = ph.sb("e1", [128, 32], F32)
        be1 = ph.sb("be1", [128, 32], F32)
        bktm, bvtm, bcols, be1b = Buf("ktm"), Buf("vtm"), Buf("cols"), Buf("e1")
        Gbc = Rot(ph, "Gbc", [128, 8, C], F32, 1)
        Bbc = Rot(ph, "Bbc", [128, 8, C], F32, 1)
        ppf = Rot(ph, "ppf", [128, 8, 128], F32, 3, dma=False, psum=True)
        ppb = Rot(ph, "ppb", [128, 8, 128], BF16, 2, dma=False, psum=True)

        def T(name, dt):
            return ph.sb(name, [128, 8, 128], dt), Buf(name)

        kq, bkq = T("kq", F32)
        EX, bEX = T("EX", F32)
        EXn, bEXn = T("EXn", F32)
        tA, btA = T("tA", F32)
        tB, btB = T("tB", F32)
        Pp = [T("P0", BF16), T("P1", BF16)]
        PTp = [T("PT0", BF16), T("PT1", BF16)]
        XT, bXT = T("XT", BF16)
        QKT, bQKT = T("QKT", BF16)
        Rv, bRv = T("Rv", BF16)
        Rk, bRk = T("Rk", BF16)
        qst, bqst = T("qst", BF16)
        Kst, bKst = T("Kst", BF16)
        wTn, bwTn = T("wTn", BF16)
        vn, bvn = T("vn", BF16)
        e2 = ph.sb("e2", [128, 8], F32)
        egl = ph.sb("egl", [128, 8], F32)
        be2, begl = Buf("e2"), Buf("egl")
        if d == 0:
            Ot = Rot(ph, "Ot", [128, 8, 128], F32, 2)
        else:
            Of = Rot(ph, "Of", [128, 8, 128], F32, 1)
            osum, bosum = T("osum", F32)
            ms = ph.sb("ms", [128, 8], F32)
            bms = Buf("ms")
            onw, bonw = T("onw", BF16)
            zc = Rot(ph, "zc", [128, 8, C], BF16, 1)
            og = Rot(ph, "og", [128, 8, C], BF16, 2)
        kv = self.GkT_.rearrange("(j p) t -> p j t", p=128)
        qv = self.GqT_.rearrange("(j p) t -> p j t", p=128)
        vv = self.GvT_.rearrange("(j p) t -> p j t", p=128)
        zv = self.GzT_.rearrange("(j p) t -> p j t", p=128)
        ov = self.GoT_.rearrange("(j p) t -> p j t", p=128)
        bg_t = self.Gbg_.tensor
        last = C - 1 if d == 0 else 0
        for s in range(len(self.T)):
            nch = self.Tc[s] // C
            ph.op("pool", lambda e: e.memset(Sf[:], 0.0), reads=[], writes=[bSf])
            ph.op("pool", lambda e: e.memset(Sb[:], 0.0), reads=[], writes=[bSb])
            order = range(nch) if d == 0 else range(nch - 1, -1, -1)
            for c in order:
                col0 = self.coff[s] + c * C
                kt, bk, ck = kTc.next()
                qt, bq, cq = qTc.next()
                vt, bv, cv = vTc.next()
                bgc, bbg, cbg = BGc.next()
                ph.dma("sp", kt[:], kv[:, :, col0:col0 + C], ck, writes=[bk])
                ph.dma("sp", qt[:], qv[:, :, col0:col0 + C], cq, writes=[bq])
                ph.dma("sp", vt[:], vv[:, :, col0:col0 + C], cv, writes=[bv])
                ph.dma("sp", bgc[:], self.Gbg_[:, col0:col0 + C], cbg, writes=[bbg])
                for q in range(2):
                    pt, bpt, _ = ppb.next()
                    ph.mm([lambda e, a=a, pt=pt, kt=kt, q=q: e.transpose(
                        out=pt[:, a, :], in_=kt[:, q * 8 + a, :], identity=ident_b[:]) for a in range(8)],
                        reads=[bk, b_idb], writes=[bpt])
                    ph.op("act", lambda e, pt=pt, q=q: e.copy(out=ktm[:, q * 8:q * 8 + 8, :], in_=pt[:]),
                          reads=[bpt], writes=[bktm])
                for q in range(4):
                    pt, bpt, _ = ppb.next()
                    ph.mm([lambda e, a=a, pt=pt, vt=vt, q=q: e.transpose(
                        out=pt[:, a, :], in_=vt[:, q * 8 + a, :], identity=ident_b[:]) for a in range(8)],
                        reads=[bv, b_idb], writes=[bpt])
                    ph.op("dve", lambda e, pt=pt, q=q: e.tensor_copy(out=vtm[:, q * 8:q * 8 + 8, :], in_=pt[:]),
                          reads=[bpt], writes=[bvtm])
                pf, bpf, _ = ppf.next()
                ph.mm([lambda e, pf=pf, bgc=bgc: e.transpose(out=pf[:, 0, :], in_=bgc[:], identity=ident_f[:])],
                      reads=[bbg, b_idf], writes=[bpf])
                ph.op("act", lambda e, pf=pf: e.copy(out=cols[:], in_=pf[:, 0, :]), reads=[bpf], writes=[bcols])
                gc0 = 64 + d * 32
                bc0 = d * 32
                ph.op("act", lambda e, gc0=gc0: e.activation(out=e1[:], in_=cols[:, gc0:gc0 + 32], func=AF.Exp),
                      reads=[bcols], writes=[be1b])
                ph.op("dve", lambda e, bc0=bc0: e.tensor_tensor(out=be1[:], in0=e1[:], in1=cols[:, bc0:bc0 + 32],
                                                                op=ALU.mult), reads=[be1b, bcols], writes=[be1b])
                for grp in range(4):
                    h0 = grp * 8
                    q0 = grp * 4
                    gb, bgb, cgb = Gbc.next()
                    bb, bbb, cbb = Bbc.next()
                    ph.dma("sp", gb[:], bass.AP(bg_t, (gc0 + h0) * TTC + col0, [[0, 128], [TTC, 8], [1, C]]), cgb, writes=[bgb])
                    ph.dma("sp", bb[:], bass.AP(bg_t, (bc0 + h0) * TTC + col0, [[0, 128], [TTC, 8], [1, C]]), cbb, writes=[bbb])
                    Gcol = cols[:, gc0 + h0:gc0 + h0 + 8]
                    Bcol = cols[:, bc0 + h0:bc0 + h0 + 8]
                    pk, bpk, _ = ppf.next()
                    ph.mm([lambda e, a=a, pk=pk, kt=kt, q0=q0: e.matmul(pk[:, a, :], lhsT=kt[:, q0 + a, :], rhs=kt[:, q0 + a, :],
                                                                 start=True, stop=True) for a in range(4)] +
                          [lambda e, a=a, pk=pk, kt=kt, qt=qt, q0=q0: e.matmul(pk[:, 4 + a, :], lhsT=kt[:, q0 + a, :],
                                                                        rhs=qt[:, q0 + a, :], start=True, stop=True)
                           for a in range(4)], reads=[bk, bq], writes=[bpk])
                    ph.op("act", lambda e, pk=pk: e.copy(out=kq[:], in_=pk[:]), reads=[bpk], writes=[bkq])
                    kk_b = rep2(kq[:, 0:4, :])
                    qk_b = rep2(kq[:, 4:8, :])
                    ph.op("dve", lambda e, gb=gb, Gcol=Gcol: e.tensor_tensor(
                        out=tA[:], in0=gb[:], in1=bc_last(Gcol, C), op=ALU.subtract), reads=[bgb, bcols], writes=[btA])
                    ph.op("dve", lambda e: e.tensor_scalar(out=tB[:], in0=tA[:], scalar1=0.0, scalar2=None, op0=ALU.min),
                          reads=[btA], writes=[btB])
                    ph.op("act", lambda e: e.activation(out=EX[:], in_=tB[:], func=AF.Exp), reads=[btB], writes=[bEX])
                    ph.op("dve", lambda e: e.tensor_scalar(out=tB[:], in0=tA[:], scalar1=-1.0, scalar2=0.0,
                                                           op0=ALU.mult, op1=ALU.min), reads=[btA], writes=[btB])
                    ph.op("act", lambda e: e.activation(out=EXn[:], in_=tB[:], func=AF.Exp), reads=[btB], writes=[bEXn])
                    P, bP = Pp[0]
                    PT, bPT = PTp[0]
                    ph.op("pool", lambda e: e.tensor_tensor(out=tA[:], in0=EXn[:], in1=bc_mid(Msn[:], 8), op=ALU.mult),
                          reads=[bEXn, bM], writes=[btA])
                    ph.op("dve", lambda e, kk_b=kk_b: e.tensor_tensor(out=v4(tA), in0=v4(tA), in1=kk_b, op=ALU.mult),
                          reads=[btA, bkq], writes=[btA])
                    ph.op("dve", lambda e, P=P, Bcol=Bcol: e.tensor_tensor(out=P[:], in0=tA[:], in1=bc_last(Bcol, C),
                                                                           op=ALU.mult), reads=[btA, bcols], writes=[bP])
                    ph.op("pool", lambda e: e.tensor_tensor(out=tB[:], in0=EX[:], in1=bc_mid(MTsn[:], 8), op=ALU.mult),
                          reads=[bEX, bM], writes=[btB])
                    ph.op("dve", lambda e, kk_b=kk_b: e.tensor_tensor(out=v4(tB), in0=v4(tB), in1=kk_b, op=ALU.mult),
                          reads=[btB, bkq], writes=[btB])
                    ph.op("dve", lambda e, PT=PT, bb=bb: e.tensor_tensor(out=PT[:], in0=tB[:], in1=bb[:], op=ALU.mult),
                          reads=[btB, bbb], writes=[bPT])
                    ph.op("pool", lambda e: e.tensor_tensor(out=tA[:], in0=EX[:], in1=bc_mid(MTi[:], 8), op=ALU.mult),
                          reads=[bEX, bL, bU], writes=[btA])
                    ph.op("dve", lambda e, qk_b=qk_b: e.tensor_tensor(out=v4(QKT), in0=v4(tA), in1=qk_b, op=ALU.mult),
                          reads=[btA, bkq], writes=[bQKT])
                    ph.op("dve", lambda e, PT=PT: e.tensor_tensor(out=XT[:], in0=PT[:], in1=bc_mid(ident_b[:], 8),
                                                                   op=ALU.add), reads=[bPT, b_idb], writes=[bXT])
                    for lvl in range(1, 7):
                        Pn, bPn = Pp[lvl % 2]
                        PTn, bPTn = PTp[lvl % 2]
                        pa, bpa, _ = ppf.next()
                        ph.mm([lambda e, a=a, pa=pa, P=P, PT=PT: e.matmul(pa[:, a, :], lhsT=PT[:, a, :], rhs=P[:, a, :],
                                                                          start=True, stop=True) for a in range(8)],
                              reads=[bP, bPT], writes=[bpa])
                        ph.op("act", lambda e, Pn=Pn, pa=pa: e.copy(out=Pn[:], in_=pa[:]), reads=[bpa], writes=[bPn])
                        if lvl < 6:
                            pb_, bpb_, _ = ppf.next()
                            ph.mm([lambda e, a=a, pb_=pb_, P=P, PT=PT: e.matmul(pb_[:, a, :], lhsT=P[:, a, :], rhs=PT[:, a, :],
                                                                                start=True, stop=True) for a in range(8)],
                                  reads=[bP, bPT], writes=[bpb_])
                            ph.op("dve", lambda e, PTn=PTn, pb_=pb_: e.tensor_copy(out=PTn[:], in_=pb_[:]),
                                  reads=[bpb_], writes=[bPTn])
                        px, bpx, _ = ppf.next()
                        ph.mm([lambda e, a=a, px=px, Pn=Pn: e.matmul(px[:, a, :], lhsT=Pn[:, a, :], rhs=XT[:, a, :],
                                                                     start=True, stop=True) for a in range(8)],
                              reads=[bPn, bXT], writes=[bpx])
                        ph.op("dve", lambda e, px=px: e.tensor_tensor(out=XT[:], in0=XT[:], in1=px[:], op=ALU.add),
                              reads=[bpx, bXT], writes=[bXT])
                        P, bP, PT, bPT = Pn, bPn, PTn, bPTn
                    ph.op("pool", lambda e, Bcol=Bcol, h0=h0: e.tensor_tensor(out=Rv[:], in0=vtm[:, h0:h0 + 8, :],
                                                                       in1=bc_last(Bcol, 128), op=ALU.mult),
                          reads=[bvtm, bcols], writes=[bRv])
                    ph.op("dve", lambda e, q0=q0, h0=h0: e.tensor_tensor(out=v4(Rk), in0=rep2(ktm[:, q0:q0 + 4, :]),
                                                           in1=v4_of(bc_last(be1[:, h0:h0 + 8], 128)), op=ALU.mult),
                          reads=[bktm, be1b], writes=[bRk])
                    ph.op("act", lambda e, gb=gb: e.activation(out=tA[:], in_=gb[:], func=AF.Exp), reads=[bgb], writes=[btA])
                    ph.op("dve", lambda e, qt=qt, q0=q0: e.tensor_tensor(out=v4(qst), in0=rep2(qt[:, q0:q0 + 4, :]), in1=v4(tA),
                                                                  op=ALU.mult), reads=[bq, btA], writes=[bqst])
                    ph.op("dve", lambda e, gb=gb, Gcol=Gcol: e.tensor_tensor(out=e2[:], in0=gb[:, :, last], in1=Gcol,
                                                                              op=ALU.subtract), reads=[bgb, bcols], writes=[be2])
                    ph.op("act", lambda e: e.activation(out=e2[:], in_=e2[:], func=AF.Exp), reads=[be2], writes=[be2])
                    ph.op("act", lambda e, gb=gb: e.activation(out=egl[:], in_=gb[:, :, last], func=AF.Exp),
                          reads=[bgb], writes=[begl])
                    ph.op("dve", lambda e, q0=q0: e.tensor_tensor(out=v4(Kst), in0=rep2(ktm[:, q0:q0 + 4, :]),
                                                           in1=v4_of(bc_last(e2[:], 128)), op=ALU.mult),
                          reads=[bktm, be2], writes=[bKst])
                    pw, bpw, _ = ppf.next()
                    ph.mm([lambda e, a=a, pw=pw: e.matmul(pw[:, a, :], lhsT=Rk[:, a, :], rhs=XT[:, a, :],
                                                          start=True, stop=True) for a in range(8)],
                          reads=[bRk, bXT], writes=[bpw])
                    ph.op("act", lambda e, pw=pw: e.activation(out=wTn[:], in_=pw[:], func=AF.Copy, scale=-1.0),
                          reads=[bpw], writes=[bwTn])
                    pv, bpv, _ = ppf.next()
                    fns = []
                    for a in range(8):
                        fns.append(lambda e, a=a, pv=pv: e.matmul(pv[:, a, :], lhsT=XT[:, a, :], rhs=Rv[:, a, :],
                                                                  start=True, stop=False))
                        fns.append(lambda e, a=a, pv=pv, h0=h0: e.matmul(pv[:, a, :], lhsT=wTn[:, a, :], rhs=Sb[:, h0 + a, :],
                                                                  start=False, stop=True))
                    ph.mm(fns, reads=[bXT, bRv, bwTn, bSb], writes=[bpv])
                    ph.op("dve", lambda e, pv=pv: e.tensor_copy(out=vn[:], in_=pv[:]), reads=[bpv], writes=[bvn])
                    po, bpo, _ = ppf.next()
                    fns = []
                    for a in range(8):
                        fns.append(lambda e, a=a, po=po, h0=h0: e.matmul(po[:, a, :], lhsT=qst[:, a, :], rhs=Sb[:, h0 + a, :],
                                                                  start=True, stop=False))
                        fns.append(lambda e, a=a, po=po: e.matmul(po[:, a, :], lhsT=QKT[:, a, :], rhs=vn[:, a, :],
                                                                  start=False, stop=True))
                    ph.mm(fns, reads=[bqst, bSb, bQKT, bvn], writes=[bpo])
                    if d == 0:
                        ot, bot, cot = Ot.next()
                        ph.op("act", lambda e, ot=ot, po=po: e.copy(out=ot[:], in_=po[:]), reads=[bpo], writes=[bot])
                        ph.dma("sp", self.Otm_[col0:col0 + C, h0 * 128:(h0 + 8) * 128], ot[:].rearrange("p a b -> p (a b)"),
                               cot, reads=[bot])
                    else:
                        oft, bof, cof = Of.next()
                        ph.dma("sp", oft[:].rearrange("p a b -> p (a b)"), self.Otm_[col0:col0 + C, h0 * 128:(h0 + 8) * 128],
                               cof, writes=[bof])
                        ph.op("dve", lambda e, oft=oft, po=po: e.tensor_tensor(out=osum[:], in0=oft[:], in1=po[:], op=ALU.add),
                              reads=[bof, bpo], writes=[bosum])
                        ph.op("pool", lambda e: e.tensor_tensor(out=tA[:], in0=osum[:], in1=osum[:], op=ALU.mult),
                              reads=[bosum], writes=[btA])
                        ph.op("dve", lambda e: e.tensor_reduce(out=ms[:], in_=tA[:], axis=AX.X, op=ALU.add),
                              reads=[btA], writes=[bms])
                        ph.op("act", lambda e: e.activation(out=ms[:], in_=ms[:], func=AF.Sqrt, scale=1.0 / 128,
                                                            bias=self._eps[:, 0:1]), reads=[bms], writes=[bms])
                        ph.op("dve", lambda e: e.reciprocal(out=ms[:], in_=ms[:]), reads=[bms], writes=[bms])
                        ph.op("dve", lambda e: e.tensor_tensor(out=osum[:], in0=osum[:], in1=bc_last(ms[:], 128), op=ALU.mult),
                              reads=[bosum, bms], writes=[bosum])
                        ph.op("pool", lambda e: e.tensor_tensor(out=onw[:], in0=osum[:], in1=bc_mid(nw[:], 8), op=ALU.mult),
                              reads=[bosum, bnw], writes=[bonw])
                        pt, bpt, _ = ppb.next()
                        ph.mm([lambda e, a=a, pt=pt: e.transpose(out=pt[:, a, :], in_=onw[:, a, :], identity=ident_b[:])
                               for a in range(8)], reads=[bonw, b_idb], writes=[bpt])
                        zt, bz, cz_ = zc.next()
                        ph.dma("sp", zt[:], zv[:, h0:h0 + 8, col0:col0 + C], cz_, writes=[bz])
                        ogt, bog, cog = og.next()
                        ph.op("dve", lambda e, ogt=ogt, pt=pt, zt=zt: e.tensor_tensor(out=ogt[:], in0=pt[:], in1=zt[:], op=ALU.mult),
                              reads=[bpt, bz], writes=[bog])
                        ph.dma("sp", ov[:, h0:h0 + 8, col0:col0 + C], ogt[:], cog, reads=[bog])
                    psn, bpsn, _ = ppf.next()
                    ph.mm([lambda e, a=a, psn=psn: e.matmul(psn[:, a, :], lhsT=Kst[:, a, :], rhs=vn[:, a, :],
                                                            start=True, stop=True) for a in range(8)],
                          reads=[bKst, bvn], writes=[bpsn])
                    ph.op("dve", lambda e, h0=h0: e.tensor_tensor(out=Sf[:, h0:h0 + 8, :], in0=Sf[:, h0:h0 + 8, :],
                                                                  in1=bc_last(egl[:], 128), op=ALU.mult),
                          reads=[bSf, begl], writes=[bSf])
                    ph.op("dve", lambda e, h0=h0, psn=psn: e.tensor_tensor(out=Sf[:, h0:h0 + 8, :], in0=Sf[:, h0:h0 + 8, :],
                                                                           in1=psn[:], op=ALU.add),
                          reads=[bSf, bpsn], writes=[bSf])
                    ph.op("act", lambda e, h0=h0: e.copy(out=Sb[:, h0:h0 + 8, :], in_=Sf[:, h0:h0 + 8, :]),
                          reads=[bSf], writes=[bSb])
        ph.finish()

    def build(self):
        self.nc = bass.Bass("TRN2", target_bir_lowering=False)
        nc = self.nc
        self.declare()
        with ExitStack() as es:
            self.sems = [es.enter_context(nc.semaphore(f"sm{i}")) for i in range(NSEM)]
            self._eps = es.enter_context(nc.sbuf_tensor("g_eps", [128, 1], F32))
            self.b_eps = Buf("eps")
            ph0 = Phase(self, "init")
            ph0.op("pool", lambda e: e.memset(self._eps[:], EPS), writes=[self.b_eps])
            self._one = es.enter_context(nc.sbuf_tensor("g_one", [128, 1], F32))
            ph0.op("pool", lambda e: e.memset(self._one[:], 1.0), writes=[Buf("one")])
            ph0.finish()
            self.b_eps = Buf("eps")
            self.phase_embed()
            for li in range(DEPTH):
                if self.stop_after is not None and li >= self.stop_after:
                    break
                self.mixer(li)
                if DEBUG_SNAP:
                    self.phase_snap(f"m{li}")
                if SKIP_FFN:
                    continue
                if ("n", li) not in PH_SKIP:
                    self.phase_norm(self.inp["ffn_norm"][li:li + 1])
                if ("a", li) not in PH_SKIP:
                    self.phase_ffn_a(li)
                if ("b", li) not in PH_SKIP:
                    self.phase_proj_res(f"fb{li}", self.inp["ffn_w_down"][li], KF, self.HT)
                if DEBUG_SNAP:
                    self.phase_snap(f"f{li}")
            self.phase_norm(self.inp["final_norm"].rearrange("(o d) -> o d", o=1), final=True)
        return nc

    def mixer(self, li):
        if MIXERS is not None and li not in MIXERS:
            return
        kind, j = li % 3, li // 3
        self.phase_norm(self.inp["mix_norm"][li:li + 1])
        if kind == 0:
            self.phase_rg_in(j)
            if RG_STAGES >= 2:
                self.phase_rg_scan(j)
            if RG_STAGES >= 3:
                self.phase_proj_res(f"ro{j}", self.inp["rg_w_out"][j], KD, self.mT)
        elif kind == 1:
            self.mixer_na(j)
        else:
            self.mixer_gdn(j)


WEIGHT_NAMES = ["meta_tokens", "mix_norm", "ffn_norm", "final_norm", "rg_w_in", "rg_conv_w", "rg_conv_b",
                "rg_w_a", "rg_b_a", "rg_w_i", "rg_b_i", "rg_lam", "rg_w_out", "na_w_qkv", "na_rpb",
                "na_meta_bias", "na_w_o", "gdn_w_in", "gdn_conv_w", "gdn_a_log", "gdn_dt_bias", "gdn_norm_w",
                "gdn_w_out", "ffn_w_gate", "ffn_w_up", "ffn_conv_w", "ffn_conv_b", "ffn_w_down"]


def run(inputs, stop_after=None):
    xp = np.ascontiguousarray(inputs["x_prompt"], dtype=np.float32)
    xs = np.ascontiguousarray(inputs["x_sample"], dtype=np.float32)
    ncores = 8
    npp = xp.shape[0] // ncores
    nsamp = xs.shape[0]
    b = Builder([xp.shape[1]] * npp + [xs.shape[1]], npp, stop_after=stop_after)
    import time as _time
    _t0 = _time.time()
    nc = b.build()
    import time as _time
    cols = np.arange(GRID_W)
    cstart = np.clip(cols - 8, 0, GRID_W - 16)
    namask = ((cols[:, None] >= cstart[None, :]) & (cols[:, None] < cstart[None, :] + 16)).astype(np.float32)
    consts = {"c_ident": np.eye(128, dtype=np.float32), "c_namask": namask,
              "c_tril": np.tril(np.ones((128, 128), np.float32)), "c_triu": np.triu(np.ones((128, 128), np.float32))}
    shared = {}
    for key in b.inp.d:
        if key in ("x_p", "x_s"):
            continue
        if key in consts:
            shared[key] = consts[key]
        elif "@" in key:
            nm, i = key.split("@")
            shared[key.replace("@", "_L")] = np.ascontiguousarray(inputs[nm][int(i)], dtype=np.float32)
        else:
            shared[key] = np.ascontiguousarray(inputs[key], dtype=np.float32)
    in_maps = []
    for c in range(ncores):
        m = dict(shared)
        m["x_p"] = xp[c * npp:(c + 1) * npp]
        m["x_s"] = xs[c % nsamp:c % nsamp + 1]
        in_maps.append(m)
    print("build done", _time.time() - _t0, "inputs MB/core", sum(v.nbytes for v in in_maps[0].values()) / 1e6, flush=True)
    res = run_bass_kernel_spmd(nc, in_maps, core_ids=list(range(ncores)))
    print("run done", _time.time() - _t0, flush=True)
    if DEBUG_GDN or DEBUG_SNAP:
        LAST["dbg"] = {k: np.asarray(v) for k, v in res.results[0].items() if k.startswith("s_")}
        LAST["builder"] = b
    yp = np.concatenate([res.results[c]["y_p"] for c in range(ncores)], axis=0)
    ys = np.concatenate([res.results[c]["y_s"] for c in range(nsamp)], axis=0)
    return yp, ys


def kernel(**inputs):
    return run(inputs)
```
